# Optimizing a Trainium2 kernel written in Bass

```python
import jax
import jax.numpy as jnp
from jax import lax
import numpy as np

D_MODEL = 1024
BATCH = 1
SEQ = 16384
DEPTH = 4

GRID_W = 64
CTX_LEN = 256
N_MIXERS = 3
N_CONV_LAYERS = (DEPTH + 2) // N_MIXERS
N_RET_LAYERS = (DEPTH + 1) // N_MIXERS
N_ATT_LAYERS = DEPTH // N_MIXERS
CONV_WIDTH = 31
RET_HEADS = 4
RET_QK_DIM = D_MODEL // RET_HEADS
RET_V_DIM = 2 * D_MODEL // RET_HEADS
RET_CHUNK = 128
ATT_Q_HEADS = 16
ATT_KV_HEADS = 4
ATT_HEAD_DIM = 64
ATT_WINDOW = 128
ATT_BLOCK = 128
FFN_HIDDEN = -(-8 * D_MODEL // (3 * 256)) * 256
ROPE_BASE = 10000.0
NORM_EPS = 1e-6
NEG_INF = -1e30

kernel_name = 'hybrid_conv_retention_swa_dit'


def rms_norm(x, g):
    xf = x.astype(jnp.float32)
    y = xf * lax.rsqrt(jnp.mean(xf * xf, axis=-1, keepdims=True) + NORM_EPS)
    return (y * g.astype(jnp.float32)).astype(x.dtype)


def layer_norm(x, g):
    xf = x.astype(jnp.float32)
    xc = xf - jnp.mean(xf, axis=-1, keepdims=True)
    y = xc * lax.rsqrt(jnp.mean(xc * xc, axis=-1, keepdims=True) + NORM_EPS)
    return (y * g.astype(jnp.float32)).astype(x.dtype)


def modulate(x, g, shift, scale):
    return rms_norm(x, g) * (1 + scale) + shift


def adaln(cond, w, b):
    mods = jnp.split(jax.nn.silu(cond) @ w + b, 6, axis=-1)
    return [m[:, None, :] for m in mods]


def grid_positions(T):
    n_rows = T // GRID_W
    rows = jnp.repeat(jnp.arange(n_rows), GRID_W).astype(jnp.float32)
    cols = (jnp.arange(n_rows * GRID_W) % GRID_W).astype(jnp.float32)
    return rows, cols


def _rotate(x, ang):
    m = x.shape[-1] // 2
    cos = jnp.cos(ang)[None, :, None, :]
    sin = jnp.sin(ang)[None, :, None, :]
    x1, x2 = x[..., :m], x[..., m:]
    return jnp.concatenate([x1 * cos - x2 * sin, x2 * cos + x1 * sin], axis=-1)


def axial_rope(x, rows, cols):
    hd = x.shape[-1]
    half, quarter = hd // 2, hd // 4
    inv = ROPE_BASE ** (-jnp.arange(quarter, dtype=jnp.float32) / quarter)
    xf = x.astype(jnp.float32)
    out = jnp.concatenate([_rotate(xf[..., :half], rows[:, None] * inv),
                           _rotate(xf[..., half:], cols[:, None] * inv)], axis=-1)
    return out.astype(x.dtype)


def conv_module(h, w1, b1, dw, dw_b, norm_g, w2):
    y = h @ w1 + b1
    a, gt = jnp.split(y, 2, axis=-1)
    y = a * jax.nn.sigmoid(gt)
    y = lax.conv_general_dilated(
        y, dw[:, None, :].astype(y.dtype), window_strides=(1,),
        padding=[(CONV_WIDTH // 2, CONV_WIDTH // 2)],
        dimension_numbers=('NWC', 'WIO', 'NWC'),
        feature_group_count=D_MODEL) + dw_b
    y = jax.nn.silu(layer_norm(y, norm_g))
    return y @ w2


def retention_chunked(q, k, v, log_gamma, s0):
    B, H, T, dk = q.shape
    dv = v.shape[-1]
    C = RET_CHUNK
    N = T // C
    idx = jnp.arange(C, dtype=jnp.float32)
    rel = idx[:, None] - idx[None, :]
    intra_decay = jnp.where(rel >= 0, jnp.exp(log_gamma[:, None, None] * jnp.maximum(rel, 0.0)), 0.0)
    q_decay = jnp.exp(log_gamma[:, None] * (idx + 1.0))[None, :, :, None]
    k_decay = jnp.exp(log_gamma[:, None] * (C - 1.0 - idx))[None, :, :, None]
    chunk_decay = jnp.exp(log_gamma * C)[None, :, None, None]
    lead = lambda t: jnp.moveaxis(t.reshape(B, H, N, C, t.shape[-1]), 2, 0)

    def step(s, xs):
        qc, kc, vc = xs
        scores = jnp.einsum('bhcd,bhmd->bhcm', qc, kc) * intra_decay
        o = jnp.einsum('bhcm,bhme->bhce', scores, vc) + jnp.einsum('bhcd,bhde->bhce', qc * q_decay, s)
        s = s * chunk_decay + jnp.einsum('bhcd,bhce->bhde', kc * k_decay, vc)
        return s, o

    s_final, o = lax.scan(step, s0, (lead(q), lead(k), lead(v)))
    return jnp.moveaxis(o, 0, 2).reshape(B, H, T, dv), s_final


def retention_state(k, v, log_gamma, reverse):
    L = k.shape[2]
    m = jnp.arange(L, dtype=jnp.float32)
    dist = m if reverse else (L - 1.0 - m)
    w = jnp.exp(log_gamma[:, None] * dist)[None, :, :, None]
    return jnp.einsum('bhld,bhle->bhde', k * w, v)


def retention_mixer(hc, hx, w_in, dec_f, dec_b, w_out, rows, cols, ctx_out):
    H, dk, dv = RET_HEADS, RET_QK_DIM, RET_V_DIM

    def project(h, rope):
        B, T, _ = h.shape
        q, k, v, g = jnp.split(h @ w_in, [H * dk, 2 * H * dk, 2 * H * dk + H * dv], axis=-1)
        q = q.reshape(B, T, H, dk)
        k = k.reshape(B, T, H, dk)
        if rope:
            q = axial_rope(q, rows, cols)
            k = axial_rope(k, rows, cols)
        heads = lambda t: jnp.swapaxes(t.astype(jnp.float32), 1, 2)
        return heads(q), heads(k) * dk ** -0.5, heads(v.reshape(B, T, H, dv)), g

    def readout(o, g):
        B, _, T, _ = o.shape
        o = o * lax.rsqrt(jnp.mean(o * o, axis=-1, keepdims=True) + NORM_EPS)
        o = jnp.swapaxes(o, 1, 2).reshape(B, T, H * dv).astype(g.dtype)
        return (jax.nn.silu(g) * o) @ w_out

    flip = lambda t: jnp.flip(t, axis=2)
    lg_f = jax.nn.log_sigmoid(dec_f.astype(jnp.float32))
    lg_b = jax.nn.log_sigmoid(dec_b.astype(jnp.float32))
    qc, kc, vc, gc = project(hc, False)
    qx, kx, vx, gx = project(hx, True)
    if ctx_out:
        zeros = jnp.zeros((hc.shape[0], H, dk, dv), jnp.float32)
        oc_f, sc_f = retention_chunked(qc, kc, vc, lg_f, zeros)
        oc_b, sc_b = retention_chunked(flip(qc), flip(kc), flip(vc), lg_b, zeros)
        oc = readout(oc_f + flip(oc_b), gc)
    else:
        sc_f = retention_state(kc, vc, lg_f, False)
        sc_b = retention_state(kc, vc, lg_b, True)
        oc = None
    ox_f, _ = retention_chunked(qx, kx, vx, lg_f, sc_f)
    ox_b, _ = retention_chunked(flip(qx), flip(kx), flip(vx), lg_b, sc_b)
    ox = readout(ox_f + flip(ox_b), gx)
    return oc, ox


def attention_mixer(hc, hx, w_qkv, q_gain, k_gain, sink, w_o, rows, cols, ctx_out):
    Hq, Hk, hd = ATT_Q_HEADS, ATT_KV_HEADS, ATT_HEAD_DIM
    G = Hq // Hk

    def project(h, rope):
        B, T, _ = h.shape
        q, k, v = jnp.split(h @ w_qkv, [Hq * hd, (Hq + Hk) * hd], axis=-1)
        q = rms_norm(q.reshape(B, T, Hq, hd), q_gain)
        k = rms_norm(k.reshape(B, T, Hk, hd), k_gain)
        if rope:
            q = axial_rope(q, rows, cols)
            k = axial_rope(k, rows, cols)
        q = (q * hd ** -0.5).reshape(B, T, Hk, G, hd)
        return q, k, v.reshape(B, T, Hk, hd)

    sink_kg = sink.astype(jnp.float32).reshape(Hk, G)[None, :, :, None, None]

    def sink_softmax(parts):
        s_sink = jnp.broadcast_to(sink_kg, parts[0].shape[:-1] + (1,))
        return jax.nn.softmax(jnp.concatenate(parts + [s_sink], axis=-1), axis=-1)

    qc, kc, vc = project(hc, False)
    qx, kx, vx = project(hx, True)
    B, L = hc.shape[0], hc.shape[1]
    T = hx.shape[1]
    C = ATT_BLOCK
    N = T // C

    if ctx_out:
        s = jnp.einsum('bqkgd,bjkd->bkgqj', qc, kc).astype(jnp.float32)
        p = sink_softmax([s])[..., :L].astype(vc.dtype)
        oc = jnp.einsum('bkgqj,bjkd->bqkgd', p, vc).reshape(B, L, Hq * hd) @ w_o
    else:
        oc = None

    pad = lambda t: jnp.pad(t, ((0, 0), (C, C), (0, 0), (0, 0))).reshape(B, N + 2, C, Hk, hd)
    band = lambda tp: jnp.concatenate([tp[:, :-2], tp[:, 1:-1], tp[:, 2:]], axis=2)
    kb = band(pad(kx))
    vb = band(pad(vx))
    qb = qx.reshape(B, N, C, Hk, G, hd)
    a = jnp.arange(C)[:, None]
    j = jnp.arange(3 * C)[None, :]
    key_pos = jnp.arange(N)[:, None, None] * C + (j - C)[None]
    mask = (jnp.abs(j - C - a) <= ATT_WINDOW)[None] & (key_pos >= 0) & (key_pos < T)

    def one_block(args):
        q_n, k_n, v_n, m_n = args
        s_ctx = jnp.einsum('bqkgd,bjkd->bkgqj', q_n, kc).astype(jnp.float32)
        s_band = jnp.where(m_n, jnp.einsum('bqkgd,bjkd->bkgqj', q_n, k_n).astype(jnp.float32), NEG_INF)
        p = sink_softmax([s_ctx, s_band])
        o = jnp.einsum('bkgqj,bjkd->bqkgd', p[..., :L].astype(vc.dtype), vc)
        return o + jnp.einsum('bkgqj,bjkd->bqkgd', p[..., L:L + 3 * C].astype(v_n.dtype), v_n)

    lead = lambda t: jnp.moveaxis(t, 1, 0)
    ob = lax.map(one_block, (lead(qb), lead(kb), lead(vb), mask))
    ox = jnp.moveaxis(ob, 0, 1).reshape(B, T, Hq * hd) @ w_o
    return oc, ox


def swiglu_ffn(h, w_gu, w_down):
    a, b = jnp.split(h @ w_gu, 2, axis=-1)
    return (jax.nn.silu(a) * b) @ w_down


def setup_inputs(seed: int = 0) -> dict:
    key = jax.random.key(seed)
    ks = iter(jax.random.split(key, 32))
    nrm = lambda shape, scale: jax.random.normal(next(ks), shape, jnp.float32) * scale
    D = D_MODEL
    ret_proj = 2 * RET_HEADS * RET_QK_DIM + 2 * RET_HEADS * RET_V_DIM
    att_proj = (ATT_Q_HEADS + 2 * ATT_KV_HEADS) * ATT_HEAD_DIM
    gam = 1.0 - np.exp(np.linspace(np.log(1.0 / 32), np.log(1.0 / 512), RET_HEADS))
    dec_logit = jnp.asarray(np.log(gam / (1.0 - gam)), jnp.float32)
    return {
        'x': nrm((BATCH, SEQ, D), 1.0),
        'c': nrm((BATCH, D), 1.0),
        'ctx': nrm((BATCH, CTX_LEN, D), 1.0),
        'c_ctx': nrm((D,), 1.0),
        'ada_w': nrm((DEPTH, D, 6 * D), 0.5 * D ** -0.5),
        'ada_b': nrm((DEPTH, 6 * D), 0.01),
        'norm_mix': 1.0 + nrm((DEPTH, D), 0.02),
        'norm_ffn': 1.0 + nrm((DEPTH, D), 0.02),
        'conv_w1': nrm((N_CONV_LAYERS, D, 2 * D), D ** -0.5),
        'conv_b1': nrm((N_CONV_LAYERS, 2 * D), 0.01),
        'conv_dw': nrm((N_CONV_LAYERS, CONV_WIDTH, D), CONV_WIDTH ** -0.5),
        'conv_dw_b': nrm((N_CONV_LAYERS, D), 0.01),
        'conv_norm': 1.0 + nrm((N_CONV_LAYERS, D), 0.02),
        'conv_w2': nrm((N_CONV_LAYERS, D, D), D ** -0.5),
        'ret_w_in': nrm((N_RET_LAYERS, D, ret_proj), D ** -0.5),
        'ret_decay_f': dec_logit[None] + nrm((N_RET_LAYERS, RET_HEADS), 0.1),
        'ret_decay_b': dec_logit[None] + nrm((N_RET_LAYERS, RET_HEADS), 0.1),
        'ret_w_out': nrm((N_RET_LAYERS, RET_HEADS * RET_V_DIM, D), (RET_HEADS * RET_V_DIM) ** -0.5),
        'att_w_qkv': nrm((N_ATT_LAYERS, D, att_proj), D ** -0.5),
        'att_q_norm': 1.0 + nrm((N_ATT_LAYERS, ATT_HEAD_DIM), 0.02),
        'att_k_norm': 1.0 + nrm((N_ATT_LAYERS, ATT_HEAD_DIM), 0.02),
        'att_sink': nrm((N_ATT_LAYERS, ATT_Q_HEADS), 0.5),
        'att_w_o': nrm((N_ATT_LAYERS, ATT_Q_HEADS * ATT_HEAD_DIM, D), (ATT_Q_HEADS * ATT_HEAD_DIM) ** -0.5),
        'ffn_w_gu': nrm((DEPTH, D, 2 * FFN_HIDDEN), D ** -0.5),
        'ffn_w_down': nrm((DEPTH, FFN_HIDDEN, D), FFN_HIDDEN ** -0.5),
    }


def reference(x, c, ctx, c_ctx, ada_w, ada_b, norm_mix, norm_ffn,
              conv_w1, conv_b1, conv_dw, conv_dw_b, conv_norm, conv_w2,
              ret_w_in, ret_decay_f, ret_decay_b, ret_w_out,
              att_w_qkv, att_q_norm, att_k_norm, att_sink, att_w_o,
              ffn_w_gu, ffn_w_down):
    rows, cols = grid_positions(x.shape[1])
    h_ctx = ctx
    cond_ctx = c_ctx[None, :]
    for i in range(DEPTH):
        kind, j, last = i % N_MIXERS, i // N_MIXERS, i == DEPTH - 1
        sh_m, sc_m, g_m, sh_f, sc_f, g_f = adaln(c, ada_w[i], ada_b[i])
        hx = modulate(x, norm_mix[i], sh_m, sc_m)
        if (not last) or kind != 0:
            csh_m, csc_m, cg_m, csh_f, csc_f, cg_f = adaln(cond_ctx, ada_w[i], ada_b[i])
            hc = modulate(h_ctx, norm_mix[i], csh_m, csc_m)
        if kind == 0:
            conv_p = (conv_w1[j], conv_b1[j], conv_dw[j], conv_dw_b[j], conv_norm[j], conv_w2[j])
            ox = conv_module(hx, *conv_p)
            oc = None if last else conv_module(hc, *conv_p)
        elif kind == 1:
            oc, ox = retention_mixer(hc, hx, ret_w_in[j], ret_decay_f[j], ret_decay_b[j], ret_w_out[j],
                                     rows, cols, not last)
        else:
            oc, ox = attention_mixer(hc, hx, att_w_qkv[j], att_q_norm[j], att_k_norm[j], att_sink[j],
                                     att_w_o[j], rows, cols, not last)
        x = x + g_m * ox
        x = x + g_f * swiglu_ffn(modulate(x, norm_ffn[i], sh_f, sc_f), ffn_w_gu[i], ffn_w_down[i])
        if not last:
            h_ctx = h_ctx + cg_m * oc
            h_ctx = h_ctx + cg_f * swiglu_ffn(modulate(h_ctx, norm_ffn[i], csh_f, csc_f), ffn_w_gu[i], ffn_w_down[i])
    return x
```

```python
import numpy as np
import concourse.bass as bass
import concourse.mybir as mybir
from concourse.bass_utils import run_bass_kernel_spmd
from contextlib import ExitStack

F32 = mybir.dt.float32
BF16 = mybir.dt.bfloat16
AF = mybir.ActivationFunctionType
ALU = mybir.AluOpType
AX = mybir.AxisListType

NCORES = 8
D = 1024
KC = 8
SEQ = 16384
T = SEQ // NCORES
TC = 256
DEPTH = 4
FH = 2816
FHC = 22
EPS = 1e-6
SAME_ENGINE_SYNC = True
NO_ROPE = False


class _Op:
    __slots__ = ("eng", "fn", "seq", "waits", "signal", "sigidx", "dma", "sem", "semval", "dwaits")

    def __init__(self, eng, fn, seq, dma):
        self.eng = eng
        self.fn = fn
        self.seq = seq
        self.waits = []
        self.dwaits = []
        self.signal = False
        self.sigidx = None
        self.dma = dma
        self.sem = None
        self.semval = None


class Prog:
    ENGS = ("pe", "act", "dve", "pool", "sp")

    def __init__(self, nc):
        self.nc = nc
        self.ops = {e: [] for e in self.ENGS}
        self.lastw = {}
        self.readers = {}
        self.waited = {e: {} for e in self.ENGS}
        self.dwaited = {e: {} for e in self.ENGS}
        self.es = ExitStack()
        self.sems = {}
        self.dma_sems = {}
        self.pending_bar = {e: [] for e in self.ENGS}
        self.dma_since_bar = []
        self.psrr = 0

    def sem(self, name):
        return self.es.enter_context(self.nc.semaphore(name))

    def sbuf(self, name, shape, dt):
        return self.es.enter_context(self.nc.sbuf_tensor(name, list(shape), dt))

    def psum(self, name, shape, dt):
        return self.es.enter_context(self.nc.psum_tensor(name, list(shape), dt))

    def _add_dep(self, o, d):
        eng = o.eng
        if d.dma:
            key = id(d.sem)
            if self.dwaited[eng].get(key, 0) >= d.semval:
                return
            self.dwaited[eng][key] = d.semval
            o.dwaits.append((d.sem, d.semval))
        else:
            if d.eng == eng and not o.dma:
                if eng == "pe" or not SAME_ENGINE_SYNC:
                    return
            if self.waited[eng].get(d.eng, -1) >= d.seq:
                return
            self.waited[eng][d.eng] = d.seq
            d.signal = True
            o.waits.append(d)

    def op(self, eng, fn, reads=(), writes=(), dma_slot=None):
        lst = self.ops[eng]
        o = _Op(eng, fn, len(lst), dma_slot is not None)
        if self.pending_bar[eng]:
            for d in self.pending_bar[eng]:
                self._add_dep(o, d)
            self.pending_bar[eng] = []
        for k in reads:
            w = self.lastw.get(k)
            if w is not None:
                self._add_dep(o, w)
        for k in writes:
            w = self.lastw.get(k)
            if w is not None:
                self._add_dep(o, w)
            for r in self.readers.get(k, ()):
                self._add_dep(o, r)
        for k in reads:
            self.readers.setdefault(k, []).append(o)
        for k in writes:
            self.lastw[k] = o
            self.readers[k] = []
        if dma_slot is not None:
            ent = self.dma_sems.get(dma_slot)
            if ent is None:
                ent = [self.sem("d_" + str(dma_slot)), 0]
                self.dma_sems[dma_slot] = ent
            ent[1] += 16
            o.sem = ent[0]
            o.semval = ent[1]
            self.dma_since_bar.append(o)
        lst.append(o)
        return o

    def barrier(self):
        lasts = [self.ops[e][-1] for e in self.ENGS if self.ops[e] and not self.ops[e][-1].dma]
        lasts = []
        for e in self.ENGS:
            for o in reversed(self.ops[e]):
                if not o.dma:
                    lasts.append(o)
                    break
        for e in self.ENGS:
            self.pending_bar[e] = list(lasts) + list(self.dma_since_bar)
        self.dma_since_bar = []
        self.lastw = {}
        self.readers = {}

    def pe(self, fn, reads=(), writes=()):
        return self.op("pe", fn, reads, writes)

    def act(self, fn, reads=(), writes=()):
        return self.op("act", fn, reads, writes)

    def dve(self, fn, reads=(), writes=()):
        return self.op("dve", fn, reads, writes)

    def pool(self, fn, reads=(), writes=()):
        return self.op("pool", fn, reads, writes)

    def dma(self, q, out, in_, slot, reads=(), writes=()):
        return self.op(q, lambda e: e.dma_start(out=out, in_=in_), reads, writes, dma_slot=slot)

    def emit(self, final_waits=()):
        nc = self.nc
        for e in self.ENGS:
            self.sems[e] = self.sem("s_" + e)
        for e in self.ENGS:
            c = 0
            for o in self.ops[e]:
                if o.signal and not o.dma:
                    c += 1
                    o.sigidx = c
        sems = self.sems
        ops = self.ops
        dma_sems = self.dma_sems

        def replay(ename, eng):
            for o in ops[ename]:
                for d in o.waits:
                    eng.wait_ge(sems[d.eng], d.sigidx)
                for (s, v) in o.dwaits:
                    eng.wait_ge(s, v)
                ins = o.fn(eng)
                if o.dma:
                    ins.then_inc(o.sem, 16)
                elif o.signal:
                    ins.then_inc(sems[ename], 1)
            if ename == "sp":
                for slot in final_waits:
                    s, v = dma_sems[slot]
                    eng.wait_ge(s, v)

        with nc.Block() as block:
            @block.tensor
            def _(t):
                replay("pe", t)

            @block.scalar
            def _(t):
                replay("act", t)

            @block.vector
            def _(t):
                replay("dve", t)

            @block.gpsimd
            def _(t):
                replay("pool", t)

            @block.sync
            def _(t):
                replay("sp", t)
        self.es.close()


def _pv_layout():
    lay = {}
    off = 0

    def add(name, n):
        nonlocal off
        lay[name] = (off, n)
        off += n

    add("cond", 2 * KC)
    for i in range(DEPTH):
        add(f"ada_b{i}", 48)
        add(f"nmix{i}", 8)
        add(f"nffn{i}", 8)
    for j in range(2):
        add(f"cb1_{j}", 16)
        add(f"cdw_{j}", 8 * 31)
        add(f"cdwb_{j}", 8)
        add(f"cnorm_{j}", 8)
    add("aqn", 1)
    add("akn", 1)
    add("asink", 16)
    add("rdec", 8)
    return lay, off


PV_LAY, PV_N = _pv_layout()
PC_N = 64 + 4 * 128 + 192 + 192
PC_TRIL, PC_TRIR, PC_ML0, PC_MR15 = 64, 192, 320, 448
PC_AROPE, PC_RROPE = 576, 768
NK = TC + 128 + T + 128
RT_RELP, RT_RELN, RT_MGE, RT_MLE, RT_C1, RT_128C, RT_COLA, RT_COLB = 0, 128, 256, 384, 512, 640, 768, 769
RT_DISTF, RT_MASKF, RT_DISTB, RT_MASKB, RT_CTXF, RT_CTXB, RT_N = 770, 778, 786, 794, 802, 803, 804
NCH = 18
TT = TC + T
W_SHAPES = {
    "conv_w1": (2, D, 2 * D), "conv_w2": (2, D, D), "ret_w_in": (1, D, 6144), "ret_w_out": (1, 2048, D),
    "att_w_qkv": (1, D, 1536), "att_w_o": (1, D, D),
}
for _i in range(DEPTH):
    W_SHAPES[f"ada_w{_i}"] = (D, 6 * D)
    W_SHAPES[f"ffn_w_gu{_i}"] = (D, 2 * FH)
    W_SHAPES[f"ffn_w_down{_i}"] = (FH, D)


def _fm(v):
    v = np.asarray(v, np.float32)
    return np.ascontiguousarray(v.reshape(-1, 128).T)


def pack_pv(inp):
    pv = np.zeros((128, PV_N), np.float32)

    def put(name, arr):
        o, n = PV_LAY[name]
        pv[:, o:o + n] = np.asarray(arr, np.float32).reshape(128, n)

    cond = np.stack([_fm(inp["c"][0]), _fm(inp["c_ctx"])], axis=-1)
    put("cond", cond)
    for i in range(DEPTH):
        put(f"ada_b{i}", _fm(inp["ada_b"][i]))
        put(f"nmix{i}", _fm(inp["norm_mix"][i]))
        put(f"nffn{i}", _fm(inp["norm_ffn"][i]))
    for j in range(2):
        put(f"cb1_{j}", _fm(inp["conv_b1"][j]))
        dw = np.asarray(inp["conv_dw"][j], np.float32)
        put(f"cdw_{j}", np.ascontiguousarray(dw.reshape(31, 8, 128).transpose(2, 1, 0)))
        put(f"cdwb_{j}", _fm(inp["conv_dw_b"][j]))
        put(f"cnorm_{j}", _fm(inp["conv_norm"][j]))
    put("aqn", np.tile(np.asarray(inp["att_q_norm"][0], np.float32), 2).reshape(128, 1))
    put("akn", np.tile(np.asarray(inp["att_k_norm"][0], np.float32), 2).reshape(128, 1))
    put("asink", np.broadcast_to(np.asarray(inp["att_sink"][0], np.float32)[None, :], (128, 16)))
    rd = np.concatenate([np.asarray(inp["ret_decay_f"][0], np.float32), np.asarray(inp["ret_decay_b"][0], np.float32)])
    put("rdec", np.broadcast_to(rd[None, :], (128, 8)))
    return pv


class Builder:
    def __init__(self, steps):
        self.steps = steps
        nc = bass.Bass("TRN2", target_bir_lowering=False)
        self.nc = nc
        self.P = Prog(nc)
        P = self.P
        di = lambda name, shape: nc.dram_tensor(name, list(shape), F32, kind="ExternalInput").ap()
        self.x_in = di("x_in", [T, D])
        self.ctx_in = di("ctx_in", [TC, D])
        self.pv_in = di("pv", [128, PV_N])
        self.ident_in = di("ident", [128, 128])
        self.pc_in = di("pc", [128, PC_N])
        self.halo_in = di("halo_in", [32, D])
        self.cst_in = di("cst", [128, 3 * 128])
        self._di = di
        self.wts = {}
        self.x_out = nc.dram_tensor("x_out", [T, D], F32, kind="ExternalOutput").ap()
        self.ctx_out = nc.dram_tensor("ctx_out", [TC, D], F32, kind="ExternalOutput").ap()

        self.xT = P.sbuf("xT", [128, KC, T], F32)
        self.cT = P.sbuf("cT", [128, KC, TC], F32)
        self.pv = P.sbuf("pvs", [128, PV_N], F32)
        self.ident = P.sbuf("ident_s", [128, 128], F32)
        self.identb = P.sbuf("identb", [128, 128], BF16)
        self.onesb = P.sbuf("onesb", [128, 128], BF16)
        self.mods = P.sbuf("mods", [128, 48, 2], F32)
        self.modA = P.sbuf("modA", [128, 2, KC, 2], F32)
        self.scond = P.sbuf("scond", [128, KC, 2], BF16)
        self.epsb = P.sbuf("epsb", [128, 1], F32)
        self.pc = P.sbuf("pcs", [128, PC_N], F32)
        self.hxT = P.sbuf("hxT", [128, KC, 32], F32)
        self.cstb = P.sbuf("cstb", [128, 3, 128], BF16)
        self.ARENA = 125 * 1024 // 4
        self.arena = P.sbuf("arena", [128, self.ARENA], F32)
        self.aoff = 0
        self.atop = self.ARENA
        self.ps = [P.psum("ps%d" % i, [128, 512], F32) for i in range(8)]

        self.blocks = [("x", b * 512, 512, 0) for b in range(4)] + [("c", 0, 256, 1)]
        self.aoff_epoch = 0

    def areset(self):
        self.P.barrier()
        self.aoff = 0
        self.atop = self.ARENA
        self.aoff_epoch = getattr(self, "aoff_epoch", 0) + 1

    def alloc(self, shape, dt, top=False):
        n = int(np.prod(shape[1:]))
        words = n if dt == F32 else (n + 1) // 2
        if top:
            self.atop -= words
            v = self.arena[:, self.atop:self.atop + words]
        else:
            assert self.aoff + words <= self.atop, ("arena overflow", self.aoff, words, self.atop)
            v = self.arena[:, self.aoff:self.aoff + words]
            self.aoff += words
        if dt != F32:
            v = v.bitcast(dt)
        if len(shape) == 3:
            v = v.rearrange("p (a b) -> p a b", b=shape[2])
        elif len(shape) == 4:
            v = v.rearrange("p (a b c) -> p a b c", b=shape[2], c=shape[3])
        return v

    def nextps(self):
        b = self.P.psrr
        self.P.psrr = (b + 1) % 6
        return b

    def nextacc(self):
        self.accrr = 1 - getattr(self, "accrr", 1)
        return 6 + self.accrr

    def pvs(self, name):
        o, n = PV_LAY[name]
        return self.pv[:, o:o + n]

    def src(self, stream):
        return {"x": self.xT, "c": self.cT, "h": self.hxT}[stream]

    def W(self, name):
        if name not in self.wts:
            self.wts[name] = self._di(name, list(W_SHAPES[name]))
        return self.wts[name]

    def prologue(self):
        P = self.P
        P.dma("sp", self.pv[:], self.pv_in, "pv", writes=["pv"])
        P.dma("sp", self.pc[:], self.pc_in, "pc", writes=["pc"])
        P.dma("pool", self.cstb[:], self.cst_in.rearrange("p (a b) -> p a b", b=128), "cstb", writes=["cstb"])
        P.dma("sp", self.ident[:], self.ident_in, "ident", writes=["ident"])
        P.dma("pool", self.identb[:], self.ident_in, "identb", writes=["identb"])
        P.dve(lambda e: e.memset(self.onesb[:], 1.0), writes=["onesb"])
        P.dve(lambda e: e.memset(self.epsb[:], EPS), writes=["epsb"])
        self.aoff = 0
        stg = [self.alloc([128, D], F32) for _ in range(2)]
        tiles = [("x", i) for i in range(T // 128)] + [("c", i) for i in range(TC // 128)]
        for n, (s, i) in enumerate(tiles):
            sl = n % 2
            srcd = self.x_in if s == "x" else self.ctx_in
            dst = self.src(s)
            P.dma("sp", stg[sl], srcd[i * 128:(i + 1) * 128, :], "stg%d" % sl, writes=[("stg", sl)])
            for half in range(2):
                b = self.nextps()
                for q in range(4):
                    kc = half * 4 + q
                    P.pe(lambda e, b=b, q=q, kc=kc, sl=sl: e.transpose(self.ps[b][:, q * 128:(q + 1) * 128],
                                                                      stg[sl][:, kc * 128:(kc + 1) * 128], self.ident[:]),
                         reads=[("stg", sl), "ident"], writes=[("ps", b)])
                outv = dst[:, half * 4:half * 4 + 4, i * 128:(i + 1) * 128]
                inv = self.ps[b][:, :].rearrange("p (a b) -> p a b", b=128)
                eng = P.dve if half == 0 else P.act
                if half == 0:
                    P.dve(lambda e, o=outv, i_=inv: e.tensor_copy(o, i_), reads=[("ps", b)], writes=[(s, "T", i)])
                else:
                    P.act(lambda e, o=outv, i_=inv: e.activation(o, i_, AF.Copy), reads=[("ps", b)], writes=[(s, "T", i, 1)])
        self.areset()

    def epilogue(self, want_ctx):
        P = self.P
        self.aoff = 0
        stg = [self.alloc([128, D], F32) for _ in range(2)]
        tiles = [("x", i) for i in range(T // 128)]
        if want_ctx:
            tiles += [("c", i) for i in range(TC // 128)]
        outs = []
        for n, (s, i) in enumerate(tiles):
            sl = n % 2
            dstd = self.x_out if s == "x" else self.ctx_out
            srcT = self.src(s)
            for half in range(2):
                b = self.nextps()
                for q in range(4):
                    kc = half * 4 + q
                    P.pe(lambda e, b=b, q=q, kc=kc, i=i, srcT=srcT: e.transpose(self.ps[b][:, q * 128:(q + 1) * 128],
                                                                               srcT[:, kc, i * 128:(i + 1) * 128], self.ident[:]),
                         reads=["ident"], writes=[("ps", b)])
                o = stg[sl][:, half * 512:(half + 1) * 512]
                if half == 0:
                    P.dve(lambda e, o=o, b=b: e.tensor_copy(o, self.ps[b][:, :]), reads=[("ps", b)], writes=[("stg", sl, 0)])
                else:
                    P.act(lambda e, o=o, b=b: e.activation(o, self.ps[b][:, :], AF.Copy), reads=[("ps", b)], writes=[("stg", sl, 1)])
            slot = "out_" + s
            P.dma("sp", dstd[i * 128:(i + 1) * 128, :], stg[sl], slot, reads=[("stg", sl, 0), ("stg", sl, 1)])
            if slot not in outs:
                outs.append(slot)
        return outs

    def load_halo(self, dram_ap):
        P = self.P
        self.aoff = 0
        stg = self.alloc([128, D], F32)
        P.dma("sp", stg[0:32, :], dram_ap, "hstg", writes=["hstg"])
        for half in range(2):
            b = self.nextps()
            for q in range(4):
                kc = half * 4 + q
                P.pe(lambda e, b=b, q=q, kc=kc: e.transpose(self.ps[b][:, q * 32:(q + 1) * 32],
                                                            stg[0:32, kc * 128:(kc + 1) * 128], self.ident[0:32, 0:32]),
                     reads=["hstg", "ident"], writes=[("ps", b)])
            P.dve(lambda e, b=b, half=half: e.tensor_copy(self.hxT[:, half * 4:half * 4 + 4, :],
                                                           self.ps[b][:, 0:128].rearrange("p (a b) -> p a b", b=32)),
                  reads=[("ps", b)], writes=[("h", "T")])
        self.areset()

    def conv(self, i, j, with_ctx):
        P = self.P
        self.aoff = 0
        TX = T + 32
        TCX = TC + 32
        gluX = self.alloc([128, KC, TX], BF16, top=True)
        gluC = self.alloc([128, KC, TCX], BF16, top=True) if with_ctx else None
        keep = 0
        blocks = [("x", b * 512, 512, 0) for b in range(4)] + [("h", 0, 32, 0)]
        if with_ctx:
            blocks.append(("c", 0, 256, 1))
        ntok = sum(b[2] for b in blocks)
        hT = self.alloc([128, KC, ntok], BF16)
        sq = self.alloc([128, KC, 512], BF16)
        tmp = self.alloc([128, 3, 512], F32)
        offs = {}
        o = 0
        for blk in blocks:
            offs[blk] = o
            o += blk[2]

        def hT_of(blk):
            return hT[:, :, offs[blk]:offs[blk] + blk[2]], ("hT", blk[0], blk[1])

        self.modulate(0, blocks, hT_of, sq, tmp)
        if with_ctx:
            P.dve(lambda e: e.memset(gluC[:, :, 0:16], 0.0), writes=[("gluC", "h")])
            P.dve(lambda e: e.memset(gluC[:, :, 16 + TC:TCX], 0.0), writes=[("gluC", "h")])
        wsl = [self.alloc([128, KC, 512], BF16) for _ in range(2)]
        sig = [self.alloc([128, 512], F32) for _ in range(2)]
        gtmp = self.alloc([128, 32], F32)
        ob1, _ = PV_LAY[f"cb1_{j}"]
        W1 = self.W("conv_w1")[j]
        groups = [[(2 * g * 128, 256), (D + 2 * g * 128, 256)] for g in range(4)]
        for gi, grp in enumerate(groups):
            sl = gi % 2
            wb = wsl[sl]
            c = 0
            for (c0, nc_) in grp:
                P.dma("pool", wb[:, :, c:c + nc_], W1[:, c0:c0 + nc_].rearrange("(kc p) n -> p kc n", p=128),
                      "wc1%d" % sl, writes=[("wc1", sl)])
                c += nc_
            for blk in blocks:
                s, t0, n, st = blk
                hv = hT[:, :, offs[blk]:offs[blk] + n]
                hk = ("hT", blk[0], blk[1])
                for q in range(2):
                    ch = 2 * gi + q
                    ba = self.nextps()
                    bg = self.nextps()
                    for kc in range(KC):
                        P.pe(lambda e, ba=ba, n=n, wb=wb, kc=kc, q=q, hv=hv: e.matmul(
                            self.ps[ba][:, 0:n], wb[:, kc, q * 128:(q + 1) * 128], hv[:, kc, :],
                            start=(kc == 0), stop=(kc == KC - 1)), reads=[("wc1", sl), hk], writes=[("ps", ba)])
                    for kc in range(KC):
                        P.pe(lambda e, bg=bg, n=n, wb=wb, kc=kc, q=q, hv=hv: e.matmul(
                            self.ps[bg][:, 0:n], wb[:, kc, 256 + q * 128:256 + (q + 1) * 128], hv[:, kc, :],
                            start=(kc == 0), stop=(kc == KC - 1)), reads=[("wc1", sl), hk], writes=[("ps", bg)])
                    sg = sig[q]
                    P.act(lambda e, bg=bg, n=n, sg=sg, ch=ch: e.activation(
                        sg[:, 0:n], self.ps[bg][:, 0:n], AF.Sigmoid, bias=self.pv[:, ob1 + 8 + ch:ob1 + 9 + ch], scale=1.0),
                        reads=[("ps", bg), "pv"], writes=[("sig", q)])
                    ba_ap = self.pv[:, ob1 + ch:ob1 + ch + 1]
                    if s == "x":
                        dst = gluX[:, ch, 16 + t0:16 + t0 + n]
                        P.dve(lambda e, ba=ba, n=n, sg=sg, dst=dst, ba_ap=ba_ap: e.scalar_tensor_tensor(
                            dst, self.ps[ba][:, 0:n], ba_ap, sg[:, 0:n], ALU.add, ALU.mult),
                            reads=[("ps", ba), ("sig", q), "pv"], writes=[("gluX", ch)])
                    elif s == "c":
                        dst = gluC[:, ch, 16:16 + n]
                        P.dve(lambda e, ba=ba, n=n, sg=sg, dst=dst, ba_ap=ba_ap: e.scalar_tensor_tensor(
                            dst, self.ps[ba][:, 0:n], ba_ap, sg[:, 0:n], ALU.add, ALU.mult),
                            reads=[("ps", ba), ("sig", q), "pv"], writes=[("gluC", ch)])
                    else:
                        P.dve(lambda e, ba=ba, n=n, sg=sg, ba_ap=ba_ap: e.scalar_tensor_tensor(
                            gtmp[:, 0:32], self.ps[ba][:, 0:n], ba_ap, sg[:, 0:n], ALU.add, ALU.mult),
                            reads=[("ps", ba), ("sig", q), "pv"], writes=["gtmp"])
                        P.dve(lambda e, ch=ch: e.tensor_tensor(gluX[:, ch, 0:16], gtmp[:, 0:16], self.pc[:, 0:16], ALU.mult),
                              reads=["gtmp", "pc"], writes=[("gluX", ch)])
                        P.dve(lambda e, ch=ch: e.tensor_tensor(gluX[:, ch, 16 + T:TX], gtmp[:, 16:32], self.pc[:, 16:32], ALU.mult),
                              reads=["gtmp", "pc"], writes=[("gluX", ch)])
        P.barrier()
        self.aoff = keep
        oblocks = [("x", b * 512, 512, 0) for b in range(4)] + ([("c", 0, 256, 1)] if with_ctx else [])
        ntok2 = sum(b[2] for b in oblocks)
        y2 = self.alloc([128, KC, ntok2], F32)
        keep2 = self.aoff
        diag = [self.alloc([128, 31, 128], BF16) for _ in range(2)]
        odw, _ = PV_LAY[f"cdw_{j}"]
        odwb, _ = PV_LAY[f"cdwb_{j}"]
        o2 = {}
        o = 0
        for blk in oblocks:
            o2[blk] = o
            o += blk[2]
        for kc in range(KC):
            dg = diag[kc % 2]
            for tap in range(31):
                P.pool(lambda e, dg=dg, tap=tap, kc=kc: e.tensor_scalar(
                    dg[:, tap, :], self.identb[:], self.pv[:, odw + kc * 31 + tap:odw + kc * 31 + tap + 1], None, ALU.mult),
                    reads=["identb", "pv"], writes=[("diag", kc % 2)])
            for blk in oblocks:
                s, t0, n, st = blk
                g = gluX if s == "x" else gluC
                b = self.nextps()
                for tap in range(31):
                    P.pe(lambda e, b=b, n=n, dg=dg, tap=tap, g=g, kc=kc, t0=t0: e.matmul(
                        self.ps[b][:, 0:n], dg[:, tap, :], g[:, kc, t0 + tap + 1:t0 + tap + 1 + n],
                        start=(tap == 0), stop=(tap == 30)), reads=[("diag", kc % 2)], writes=[("ps", b)])
                P.act(lambda e, b=b, n=n, blk=blk, kc=kc: e.activation(
                    y2[:, kc, o2[blk]:o2[blk] + n], self.ps[b][:, 0:n], AF.Identity,
                    bias=self.pv[:, odwb + kc:odwb + kc + 1], scale=1.0), reads=[("ps", b), "pv"], writes=[("y2", blk[0], blk[1])])
        P.barrier()
        self.aoff = keep2
        self.atop = self.ARENA
        w2 = self.alloc([128, KC, D], BF16)
        sq = self.alloc([128, KC, 512], BF16)
        yb = self.alloc([128, KC, 512], BF16)
        zT = self.alloc([128, KC, 512], BF16)
        st_ = self.alloc([128, 5, 512], F32)
        ocn, _ = PV_LAY[f"cnorm_{j}"]
        P.dma("pool", w2[:, :, :], self.W("conv_w2")[j].rearrange("(kc p) n -> p kc n", p=128), "wc2", writes=["wc2"])
        for blk in oblocks:
            s, t0, n, st = blk
            yv = y2[:, :, o2[blk]:o2[blk] + n]
            P.act(lambda e, yv=yv, n=n: e.activation(sq[:, :, 0:n], yv, AF.Square), writes=["sq2"])
            P.dve(lambda e, yv=yv, n=n: e.tensor_copy(yb[:, :, 0:n], yv), writes=["yb"])
            bm = self.nextps()
            bs = self.nextps()
            for kc in range(KC):
                P.pe(lambda e, bm=bm, kc=kc, n=n: e.matmul(self.ps[bm][:, 0:n], self.onesb[:], yb[:, kc, 0:n],
                                                            start=(kc == 0), stop=(kc == KC - 1)), reads=["yb"], writes=[("ps", bm)])
            for kc in range(KC):
                P.pe(lambda e, bs=bs, kc=kc, n=n: e.matmul(self.ps[bs][:, 0:n], self.onesb[:], sq[:, kc, 0:n],
                                                            start=(kc == 0), stop=(kc == KC - 1)), reads=["sq2"], writes=[("ps", bs)])
            mean = st_[:, 0, 0:n]
            m2 = st_[:, 1, 0:n]
            var = st_[:, 2, 0:n]
            P.dve(lambda e, bm=bm, n=n, mean=mean: e.tensor_scalar(mean, self.ps[bm][:, 0:n], 1.0 / D, None, ALU.mult),
                  reads=[("ps", bm)], writes=["mean"])
            P.dve(lambda e, mean=mean, m2=m2: e.tensor_tensor(m2, mean, mean, ALU.mult), reads=["mean"], writes=["m2"])
            P.dve(lambda e, bs=bs, n=n, m2=m2, var=var: e.scalar_tensor_tensor(var, self.ps[bs][:, 0:n], 1.0 / D, m2, ALU.mult, ALU.subtract),
                  reads=[("ps", bs), "m2"], writes=["var"])
            P.act(lambda e, var=var: e.activation(var, var, AF.Sqrt, bias=self.epsb[:, 0:1], scale=1.0), reads=["var"], writes=["var"])
            P.dve(lambda e, var=var: e.reciprocal(var, var), reads=["var"], writes=["var"])
            for kc in range(KC):
                t1 = st_[:, 3 + kc % 2, 0:n]
                tk = ("t1", kc % 2)
                P.dve(lambda e, kc=kc, yv=yv, mean=mean, t1=t1: e.tensor_tensor(t1, yv[:, kc, :], mean, ALU.subtract),
                      reads=["mean"], writes=[tk])
                P.dve(lambda e, kc=kc, t1=t1, var=var: e.scalar_tensor_tensor(
                    t1, t1, self.pv[:, ocn + kc:ocn + kc + 1], var, ALU.mult, ALU.mult), reads=[tk, "var", "pv"], writes=[tk])
                P.act(lambda e, kc=kc, t1=t1, n=n: e.activation(zT[:, kc, 0:n], t1, AF.Silu), reads=[tk], writes=["zT"])
            xs = self.src(s)
            for dc in range(KC):
                bo = self.nextps()
                for kc in range(KC):
                    P.pe(lambda e, bo=bo, n=n, kc=kc, dc=dc: e.matmul(
                        self.ps[bo][:, 0:n], w2[:, kc, dc * 128:(dc + 1) * 128], zT[:, kc, 0:n],
                        start=(kc == 0), stop=(kc == KC - 1)), reads=["wc2", "zT"], writes=[("ps", bo)])
                m = 2 * 8 + dc
                P.dve(lambda e, bo=bo, n=n, xs=xs, dc=dc, t0=t0, m=m, st=st: e.scalar_tensor_tensor(
                    xs[:, dc, t0:t0 + n], self.ps[bo][:, 0:n], self.mods[:, m, st:st + 1], xs[:, dc, t0:t0 + n],
                    ALU.mult, ALU.add), reads=[("ps", bo), ("mods", m)], writes=[(s, "T")])
        self.areset()

    def qk_norm_rope(self, b, n, gain_ap, rope, dst, dkey, R, hd, swap_idx):
        P = self.P
        sqk, rs, kn, knb, t1 = R["sqk"], R["rs"], R["kn"], R["knb"], R["t1"]
        P.act(lambda e: e.activation(sqk[:, 0:n], self.ps[b][:, 0:n], AF.Square), reads=[("ps", b)], writes=["sqk"])
        b2 = self.nextps()
        P.pe(lambda e: e.matmul(self.ps[b2][:, 0:n], self.cstb[:, 0, :], sqk[:, 0:n], start=True, stop=True),
             reads=["sqk", "cstb"], writes=[("ps", b2)])
        P.act(lambda e: e.activation(rs[:, 0:n], self.ps[b2][:, 0:n], AF.Sqrt, bias=self.epsb[:, 0:1], scale=1.0 / hd),
              reads=[("ps", b2)], writes=["rs"])
        P.dve(lambda e: e.reciprocal(rs[:, 0:n], rs[:, 0:n]), reads=["rs"], writes=["rs"])
        if not rope:
            P.dve(lambda e: e.scalar_tensor_tensor(dst, self.ps[b][:, 0:n], gain_ap, rs[:, 0:n], ALU.mult, ALU.mult),
                  reads=[("ps", b), "rs"], writes=[dkey])
            return
        P.dve(lambda e: e.scalar_tensor_tensor(kn[:, 0:n], self.ps[b][:, 0:n], gain_ap, rs[:, 0:n], ALU.mult, ALU.mult),
              reads=[("ps", b), "rs"], writes=["kn"])
        self.rope_apply(kn, n, dst, dkey, R, swap_idx)

    def rope_apply(self, kn, n, dst, dkey, R, swap_idx, Cb=None, Sb=None, ckeys=("Cb",), three=False):
        P = self.P
        knb, t1 = R["knb"], R["t1"]
        Cb = R["Cb"] if Cb is None else Cb
        Sb = R["Sb"] if Sb is None else Sb
        P.act(lambda e: e.activation(knb[:, 0:n], kn[:, 0:n], AF.Copy), reads=["kn"], writes=["knb"])
        b3 = self.nextps()
        P.pe(lambda e: e.matmul(self.ps[b3][:, 0:n], self.cstb[:, swap_idx, :], knb[:, 0:n], start=True, stop=True),
             reads=["knb", "cstb"], writes=[("ps", b3)])
        v3 = (lambda a: a.rearrange("p (a b) -> p a b", b=64)) if three else (lambda a: a)
        P.pool(lambda e: e.tensor_tensor(v3(t1[:, 0:n]), v3(kn[:, 0:n]), Cb, ALU.mult), reads=["kn"] + list(ckeys), writes=["t1"])
        P.dve(lambda e: e.tensor_tensor(v3(kn[:, 0:n]), v3(self.ps[b3][:, 0:n]), Sb, ALU.mult), reads=[("ps", b3)] + list(ckeys), writes=["kn"])
        P.dve(lambda e: e.tensor_tensor(dst, kn[:, 0:n], t1[:, 0:n], ALU.add), reads=["kn", "t1"], writes=[dkey])

    def rope_tables_att(self, R, blk_idx):
        P = self.P
        o = PC_AROPE
        r0 = blk_idx * 8
        for name, ro, co in (("Cb", o, o + 64), ("Sb", o + 32, o + 128)):
            dstv = R[name].rearrange("p (a b) -> p a b", b=64)
            rowv = self.pc[:, ro + r0:ro + r0 + 8].unsqueeze(2).to_broadcast([128, 8, 64])
            colv = self.pc[:, co:co + 64].unsqueeze(1).to_broadcast([128, 8, 64])
            P.dve(lambda e, dstv=dstv, rowv=rowv, colv=colv: e.tensor_tensor(dstv, rowv, colv, ALU.add),
                  reads=["pc"], writes=["Cb"])

    def att_alloc_common(self):
        R = {}
        R["hT"] = self.alloc([128, KC, 512], BF16)
        R["sq"] = self.alloc([128, KC, 512], BF16)
        R["tmp"] = self.alloc([128, 3, 512], F32)
        R["Cb"] = self.alloc([128, 512], F32)
        R["Sb"] = self.alloc([128, 512], F32)
        R["sqk"] = self.alloc([128, 512], BF16)
        R["rs"] = self.alloc([128, 512], F32)
        R["kn"] = self.alloc([128, 512], F32)
        R["knb"] = self.alloc([128, 512], BF16)
        R["t1"] = self.alloc([128, 512], F32)
        return R

    def att_kv(self, i, bnd_out):
        P = self.P
        self.aoff = 0
        self.att_kT = self.alloc([128, 2, NK], BF16, top=True)
        self.att_vA = self.alloc([128, 20, 260], BF16, top=True)
        self.att_es = self.alloc([128, 16], F32, top=True)
        self.att_gq = self.alloc([128, 1], F32, top=True)
        kT, vA = self.att_kT, self.att_vA
        R = self.att_alloc_common()
        wk = self.alloc([128, KC, 256], BF16)
        wv = self.alloc([128, KC, 256], BF16)
        Wqkv = self.W("att_w_qkv")[0]
        P.dma("pool", wk[:, :, :], Wqkv[:, 1024:1280].rearrange("(kc p) n -> p kc n", p=128), "wk", writes=["wk"])
        P.dma("pool", wv[:, :, :], Wqkv[:, 1280:1536].rearrange("(kc p) n -> p kc n", p=128), "wv", writes=["wv"])
        osk, _ = PV_LAY["asink"]
        oq, _ = PV_LAY["aqn"]
        okn, _ = PV_LAY["akn"]
        P.act(lambda e: e.activation(self.att_es[:, :], self.pv[:, osk:osk + 16], AF.Exp), reads=["pv"], writes=["es"])
        P.dve(lambda e: e.tensor_scalar(self.att_gq[:, :], self.pv[:, oq:oq + 1], 0.125, None, ALU.mult), reads=["pv"], writes=["gq"])
        P.dve(lambda e: e.memset(vA[:, :, :], 1.0), writes=["vA"])
        for bi, blk in enumerate(self.blocks):
            s, t0, n, st = blk
            self.modulate(0, [blk], lambda b_: (R["hT"][:, :, 0:n], "hTb"), R["sq"], R["tmp"])
            rope = (s == "x") and not NO_ROPE
            if rope:
                self.rope_tables_att(R, bi)
            k0 = (TC + 128 + t0) if s == "x" else 0
            for c in range(2):
                b = self.nextps()
                for kc in range(KC):
                    P.pe(lambda e, b=b, kc=kc, c=c, n=n: e.matmul(self.ps[b][:, 0:n], wk[:, kc, c * 128:(c + 1) * 128],
                                                                 R["hT"][:, kc, 0:n], start=(kc == 0), stop=(kc == KC - 1)),
                         reads=["wk", "hTb"], writes=[("ps", b)])
                self.qk_norm_rope(b, n, self.pv[:, okn:okn + 1], rope, kT[:, c, k0:k0 + n], "kT", R, 64.0, 1)
            for tt in range(n // 128):
                b = self.nextps()
                for kc in range(KC):
                    P.pe(lambda e, b=b, kc=kc, tt=tt: e.matmul(self.ps[b][:, 0:256], R["hT"][:, kc, tt * 128:(tt + 1) * 128],
                                                              wv[:, kc, :], start=(kc == 0), stop=(kc == KC - 1)),
                         reads=["wv", "hTb"], writes=[("ps", b)])
                ti = (3 + t0 // 128 + tt) if s == "x" else tt
                dv = vA[:, ti, :].rearrange("p (h e) -> p h e", e=65)[:, :, 0:64]
                P.act(lambda e, b=b, dv=dv: e.activation(dv, self.ps[b][:, 0:256].rearrange("p (h e) -> p h e", e=64), AF.Copy),
                      reads=[("ps", b)], writes=["vA"])
        if bnd_out is not None:
            for c in range(2):
                for w, col in ((0, TC + 128), (1, TC + 128 + T - 128)):
                    P.dma("sp", bnd_out[:, c * 256 + w * 128:c * 256 + (w + 1) * 128], kT[:, c, col:col + 128], "bnd", reads=["kT"])
            for w, ti in ((0, 3), (1, 18)):
                P.dma("sp", bnd_out[:, 512 + w * 260:512 + (w + 1) * 260], vA[:, ti, :], "bnd", reads=["vA"])
        P.barrier()
        self.aoff = 0

    def att_load_halo(self, halo):
        P = self.P
        kT, vA = self.att_kT, self.att_vA
        for c in range(2):
            for w, col in ((0, TC), (1, TC + 128 + T)):
                P.dma("sp", kT[:, c, col:col + 128], halo[:, c * 256 + w * 128:c * 256 + (w + 1) * 128], "ahalo", writes=["kT"])
        for w, ti in ((0, 2), (1, 19)):
            P.dma("sp", vA[:, ti, :], halo[:, 512 + w * 260:512 + (w + 1) * 260], "ahalo", writes=["vA"])

    def att_q(self, i, with_ctx):
        P = self.P
        kT, vA = self.att_kT, self.att_vA
        R = self.att_alloc_common()
        wq = self.alloc([128, KC, 1024], BF16)
        wo = self.alloc([128, KC, 1024], BF16)
        qT = self.alloc([128, KC, 512], BF16)
        aT = self.alloc([128, KC, 512], BF16)
        pT = [self.alloc([128, 5, 128], BF16) for _ in range(2)]
        atm = [self.alloc([128, 1024], BF16) for _ in range(2)]
        den = [self.alloc([128, 4], F32) for _ in range(2)]
        Wqkv = self.W("att_w_qkv")[0]
        for ch in range(8):
            for half in range(2):
                hq = (ch // 4) * 8 + half * 4 + ch % 4
                P.dma("pool", wq[:, :, ch * 128 + half * 64:ch * 128 + half * 64 + 64],
                      Wqkv[:, hq * 64:hq * 64 + 64].rearrange("(kc p) n -> p kc n", p=128), "wq", writes=["wq"])
        P.dma("pool", wo[:, :, :], self.W("att_w_o")[0].rearrange("(kc p) n -> p kc n", p=128), "wo", writes=["wo"])
        blocks = self.blocks if with_ctx else self.blocks[:4]
        cnt = 0
        for bi, blk in enumerate(blocks):
            s, t0, n, st = blk
            self.modulate(0, [blk], lambda b_: (R["hT"][:, :, 0:n], "hTb"), R["sq"], R["tmp"])
            rope = (s == "x") and not NO_ROPE
            if rope:
                self.rope_tables_att(R, bi)
            for ch in range(8):
                b = self.nextps()
                for kc in range(KC):
                    P.pe(lambda e, b=b, kc=kc, ch=ch, n=n: e.matmul(self.ps[b][:, 0:n], wq[:, kc, ch * 128:(ch + 1) * 128],
                                                                   R["hT"][:, kc, 0:n], start=(kc == 0), stop=(kc == KC - 1)),
                         reads=["wq", "hTb"], writes=[("ps", b)])
                self.qk_norm_rope(b, n, self.att_gq[:, 0:1], rope, qT[:, ch, 0:n], ("qT", ch), R, 64.0, 1)
            for sb in range(n // 128):
                nsub = t0 // 128 + sb
                at = atm[sb % 2]
                ak = ("atm", sb % 2)
                if s == "x":
                    tiles = [0, 1, 2 + nsub, 3 + nsub, 4 + nsub]
                else:
                    tiles = [0, 1]
                nt = len(tiles)
                for h in range(4):
                    bo = self.nextacc()
                    kc_, base = h // 2, (h % 2) * 64
                    for g in range(4):
                        hq = 4 * h + g
                        qch = (h // 2) * 4 + g
                        psl = cnt % 2
                        cnt += 1
                        pt = pT[psl]
                        pk = ("pT", psl)
                        qv = qT[base:base + 64, qch, sb * 128:(sb + 1) * 128]
                        bc = self.nextps()
                        for ti in range(2):
                            P.pe(lambda e, bc=bc, ti=ti, kc_=kc_, base=base, qv=qv: e.matmul(
                                self.ps[bc][:, ti * 128:(ti + 1) * 128], kT[base:base + 64, kc_, ti * 128:(ti + 1) * 128], qv,
                                start=True, stop=True), reads=["kT", ("qT", qch)], writes=[("ps", bc)])
                        P.act(lambda e, bc=bc, pt=pt: e.activation(
                            pt[:, 0:2, :], self.ps[bc][:, 0:256].rearrange("p (a b) -> p a b", b=128), AF.Exp),
                            reads=[("ps", bc)], writes=[pk])
                        if nt == 5:
                            bb = self.nextps()
                            for q3, ti in enumerate(tiles[2:]):
                                P.pe(lambda e, bb=bb, q3=q3, ti=ti, kc_=kc_, base=base, qv=qv: e.matmul(
                                    self.ps[bb][:, q3 * 128:(q3 + 1) * 128], kT[base:base + 64, kc_, ti * 128:(ti + 1) * 128], qv,
                                    start=True, stop=True), reads=["kT", ("qT", qch)], writes=[("ps", bb)])
                            P.act(lambda e, bb=bb, pt=pt: e.activation(
                                pt[:, 2:5, :], self.ps[bb][:, 0:384].rearrange("p (a b) -> p a b", b=128), AF.Exp),
                                reads=[("ps", bb)], writes=[pk])
                            ml = PC_ML0 if nsub == 0 else PC_TRIL
                            mr = PC_MR15 if nsub == 15 else PC_TRIR
                            P.dve(lambda e, pt=pt, ml=ml: e.tensor_tensor(pt[:, 2, :], pt[:, 2, :], self.pc[:, ml:ml + 128], ALU.mult),
                                   reads=[pk, "pc"], writes=[pk])
                            P.dve(lambda e, pt=pt, mr=mr: e.tensor_tensor(pt[:, 4, :], pt[:, 4, :], self.pc[:, mr:mr + 128], ALU.mult),
                                   reads=[pk, "pc"], writes=[pk])
                        for q5, ti in enumerate(tiles):
                            P.pe(lambda e, bo=bo, g=g, q5=q5, ti=ti, pt=pt, h=h, nt=nt: e.matmul(
                                self.ps[bo][:, g * 65:(g + 1) * 65], pt[:, q5, :], vA[:, ti, h * 65:(h + 1) * 65],
                                start=(q5 == 0), stop=(q5 == nt - 1)), reads=[pk, "vA"], writes=[("ps", bo)])
                    dn = den[h % 2]
                    dk = ("den", h % 2)
                    pv4 = self.ps[bo][:, 0:260].rearrange("p (g e) -> p g e", e=65)
                    P.dve(lambda e, dn=dn, pv4=pv4, h=h: e.tensor_tensor(dn[:, :], pv4[:, :, 64], self.att_es[:, 4 * h:4 * h + 4], ALU.add),
                          reads=[("ps", bo), "es"], writes=[dk])
                    P.dve(lambda e, dn=dn: e.reciprocal(dn[:, :], dn[:, :]), reads=[dk], writes=[dk])
                    P.dve(lambda e, dn=dn, pv4=pv4, at=at, h=h: e.tensor_tensor(
                        at[:, h * 256:(h + 1) * 256].rearrange("p (g e) -> p g e", e=64), pv4[:, :, 0:64],
                        dn[:, :].unsqueeze(2).to_broadcast([128, 4, 64]), ALU.mult), reads=[("ps", bo), dk], writes=[ak])
                bt = self.nextps()
                ptb = self.ps[bt][:, :].bitcast(BF16)
                for c in range(8):
                    P.pe(lambda e, ptb=ptb, c=c, at=at: e.transpose(ptb[:, c * 128:(c + 1) * 128], at[:, c * 128:(c + 1) * 128], self.identb[:]),
                         reads=[ak, "identb"], writes=[("ps", bt)])
                P.act(lambda e, ptb=ptb, sb=sb: e.activation(aT[:, :, sb * 128:(sb + 1) * 128],
                                                              ptb[:, 0:1024].rearrange("p (a b) -> p a b", b=128), AF.Copy),
                      reads=[("ps", bt)], writes=["aT"])
            xs = self.src(s)
            for dc in range(KC):
                bo = self.nextps()
                for c in range(KC):
                    P.pe(lambda e, bo=bo, n=n, c=c, dc=dc: e.matmul(self.ps[bo][:, 0:n], wo[:, c, dc * 128:(dc + 1) * 128], aT[:, c, 0:n],
                                                                   start=(c == 0), stop=(c == KC - 1)), reads=["wo", "aT"], writes=[("ps", bo)])
                m = 2 * 8 + dc
                P.dve(lambda e, bo=bo, n=n, xs=xs, dc=dc, t0=t0, m=m, st=st: e.scalar_tensor_tensor(
                    xs[:, dc, t0:t0 + n], self.ps[bo][:, 0:n], self.mods[:, m, st:st + 1], xs[:, dc, t0:t0 + n],
                    ALU.mult, ALU.add), reads=[("ps", bo), ("mods", m)], writes=[(s, "T")])
        self.areset()

    def ret_dram(self):
        nc = self.nc
        if hasattr(self, "r_qT"):
            return
        mk = lambda name, shape, dt=BF16: nc.dram_tensor(name, list(shape), dt).ap()
        self.r_qT = mk("r_qT", [128, 8, TT])
        self.r_kT = mk("r_kT", [128, 8, TT])
        self.r_v = mk("r_v", [NCH, 128, 2048])
        self.r_g = mk("r_g", [NCH, 128, 2048])
        self.r_S = mk("r_S", [2, NCH, 128, 4096])

    def ret_tables(self):
        P = self.P
        self.aoff = 0
        A = lambda shape, dt=F32: self.alloc(shape, dt, top=True)
        self.r_Df = A([128, 4, 128])
        self.r_Db = A([128, 4, 128])
        self.r_Dc = A([128, 4, 128])
        self.r_keep_top = self.atop
        self.rt = A([128, RT_N])
        self.r_lg = A([128, 8])
        self.r_wk = A([128, 8])
        self.r_gC = A([128, 8])
        self.r_cF = A([128, 4, 8])
        self.r_cB = A([128, 4, 8])
        self.r_cc = A([128, 8])
        self.r_Sf = A([128, 4, 2, 512])
        self.r_Sb = A([128, 4, 2, 512])
        tA = self.alloc([128, 128], F32)
        tB = self.alloc([128, 128], F32)
        rt, lg = self.rt, self.r_lg
        P.dma("sp", rt[:, :], self.rt_in, "rt", writes=["rt"])
        od, _ = PV_LAY["rdec"]
        P.act(lambda e: e.activation(lg[:, :], self.pv[:, od:od + 8], AF.Exp, scale=-1.0), reads=["pv"], writes=["lg"])
        P.dve(lambda e: e.tensor_scalar(lg[:, :], lg[:, :], 1.0, None, ALU.add), reads=["lg"], writes=["lg"])
        P.act(lambda e: e.activation(lg[:, :], lg[:, :], AF.Ln), reads=["lg"], writes=["lg"])
        P.dve(lambda e: e.tensor_scalar(lg[:, :], lg[:, :], -1.0, None, ALU.mult), reads=["lg"], writes=["lg"])
        ex = lambda out, in_, col: P.act(lambda e: e.activation(out, in_, AF.Exp, scale=lg[:, col:col + 1]),
                                          reads=["lg", "rt"], writes=["rtab"])
        for h in range(4):
            ex(self.r_wk[:, h:h + 1], rt[:, RT_COLA:RT_COLA + 1], h)
            ex(self.r_wk[:, 4 + h:5 + h], rt[:, RT_COLB:RT_COLB + 1], 4 + h)
            ex(self.r_Df[:, h, :], rt[:, RT_C1:RT_C1 + 128], h)
            ex(self.r_Db[:, h, :], rt[:, RT_128C:RT_128C + 128], 4 + h)
            ex(self.r_cF[:, h, :], rt[:, RT_DISTF:RT_DISTF + 8], h)
            ex(self.r_cB[:, h, :], rt[:, RT_DISTB:RT_DISTB + 8], 4 + h)
            ex(self.r_cc[:, h:h + 1], rt[:, RT_CTXF:RT_CTXF + 1], h)
            ex(self.r_cc[:, 4 + h:5 + h], rt[:, RT_CTXB:RT_CTXB + 1], 4 + h)
            P.dve(lambda e, h=h: e.tensor_tensor(self.r_cF[:, h, :], self.r_cF[:, h, :], rt[:, RT_MASKF:RT_MASKF + 8], ALU.mult),
                  reads=["rtab", "rt"], writes=["rtab"])
            P.dve(lambda e, h=h: e.tensor_tensor(self.r_cB[:, h, :], self.r_cB[:, h, :], rt[:, RT_MASKB:RT_MASKB + 8], ALU.mult),
                  reads=["rtab", "rt"], writes=["rtab"])
            P.act(lambda e, h=h: e.activation(tA[:, :], rt[:, RT_RELP:RT_RELP + 128], AF.Exp, scale=lg[:, h:h + 1]),
                  reads=["lg", "rt"], writes=["tA"])
            P.dve(lambda e: e.tensor_tensor(tA[:, :], tA[:, :], rt[:, RT_MGE:RT_MGE + 128], ALU.mult), reads=["tA", "rt"], writes=["tA"])
            P.act(lambda e, h=h: e.activation(tB[:, :], rt[:, RT_RELN:RT_RELN + 128], AF.Exp, scale=lg[:, 4 + h:5 + h]),
                  reads=["lg", "rt"], writes=["tB"])
            P.dve(lambda e: e.tensor_tensor(tB[:, :], tB[:, :], rt[:, RT_MLE:RT_MLE + 128], ALU.mult), reads=["tB", "rt"], writes=["tB"])
            P.dve(lambda e, h=h: e.tensor_tensor(self.r_Dc[:, h, :], tA[:, :], tB[:, :], ALU.add), reads=["tA", "tB"], writes=["rtab"])
        for col in range(8):
            P.act(lambda e, col=col: e.activation(self.r_gC[:, col:col + 1], lg[:, col:col + 1], AF.Exp, scale=128.0),
                  reads=["lg"], writes=["rtab"])
        P.barrier()
        self.aoff = 0

    def ret_proj(self, kv_only):
        P = self.P
        self.ret_dram()
        self.aoff = 0
        blocks = self.blocks
        hT = self.alloc([128, KC, TT], BF16)
        sq = self.alloc([128, KC, 512], BF16)
        tmp = self.alloc([128, 3, 512], F32)

        def hcol(blk):
            return blk[1] if blk[0] == "x" else T

        self.modulate(0, blocks, lambda blk: (hT[:, :, hcol(blk):hcol(blk) + blk[2]], ("hT", blk[0], blk[1])), sq, tmp)
        R = {"knb": self.alloc([128, 512], BF16), "t1": self.alloc([128, 512], F32)}
        kn = self.alloc([128, 512], F32)
        stg = [self.alloc([128, 512], BF16) for _ in range(4)]
        wsl = [self.alloc([128, KC, 512], BF16) for _ in range(2)]
        Win = self.W("ret_w_in")[0]
        scnt = 0
        o = PC_RROPE
        fm_groups = ([] if kv_only else [0, 1]) + [2, 3]
        for gi, g in enumerate(fm_groups):
            sl = gi % 2
            wb = wsl[sl]
            P.dma("pool", wb[:, :, :], Win[:, g * 512:(g + 1) * 512].rearrange("(kc p) n -> p kc n", p=128), "rw%d" % sl, writes=[("rw", sl)])
            for bi, blk in enumerate(blocks):
                s, t0, n, st = blk
                hc = hcol(blk)
                for ci in range(4):
                    c = g * 4 + ci
                    is_k = c >= 8
                    dcx = c % 2
                    b = self.nextps()
                    for kc in range(KC):
                        P.pe(lambda e, b=b, kc=kc, ci=ci, n=n, wb=wb, hc=hc: e.matmul(
                            self.ps[b][:, 0:n], wb[:, kc, ci * 128:(ci + 1) * 128], hT[:, kc, hc:hc + n],
                            start=(kc == 0), stop=(kc == KC - 1)), reads=[("rw", sl), ("hT", s, t0)], writes=[("ps", b)])
                    sg = stg[scnt % 4]
                    sk = ("stg", scnt % 4)
                    slot = "rst%d" % (scnt % 4)
                    scnt += 1
                    scale = 0.0625 if is_k else 1.0
                    if s == "c":
                        P.act(lambda e, b=b, n=n, sg=sg, scale=scale: e.activation(sg[:, 0:n], self.ps[b][:, 0:n], AF.Copy, scale=scale),
                              reads=[("ps", b)], writes=[sk])
                        tok0 = 0
                    else:
                        P.act(lambda e, b=b, n=n, scale=scale: e.activation(kn[:, 0:n], self.ps[b][:, 0:n], AF.Copy, scale=scale),
                              reads=[("ps", b)], writes=["kn"])
                        r0 = bi * 8
                        if dcx == 0:
                            Cb = self.pc[:, o + r0:o + r0 + 8].unsqueeze(2).to_broadcast([128, 8, 64])
                            Sb = self.pc[:, o + 32 + r0:o + 32 + r0 + 8].unsqueeze(2).to_broadcast([128, 8, 64])
                        else:
                            Cb = self.pc[:, o + 64:o + 128].unsqueeze(1).to_broadcast([128, 8, 64])
                            Sb = self.pc[:, o + 128:o + 192].unsqueeze(1).to_broadcast([128, 8, 64])
                        self.rope_apply(kn, n, sg[:, 0:n], sk, R, 2, Cb=Cb, Sb=Sb, ckeys=("pc",), three=True)
                        tok0 = TC + t0
                    dst = (self.r_kT if is_k else self.r_qT)[:, c % 8, tok0:tok0 + n]
                    P.dma("sp", dst, sg[:, 0:n], slot, reads=[sk])
        tm_groups = [4, 5, 6, 7] + ([] if kv_only else [8, 9, 10, 11])
        for gi, g in enumerate(tm_groups):
            sl = gi % 2
            wb = wsl[sl]
            P.dma("pool", wb[:, :, :], Win[:, g * 512:(g + 1) * 512].rearrange("(kc p) n -> p kc n", p=128), "rw%d" % sl, writes=[("rw", sl)])
            is_g = g >= 8
            for ch in range(NCH):
                hc = (T + ch * 128) if ch < 2 else (ch - 2) * 128
                hk = ("hT", "c", 0) if ch < 2 else ("hT", "x", ((ch - 2) // 4) * 512)
                b = self.nextps()
                for kc in range(KC):
                    P.pe(lambda e, b=b, kc=kc, wb=wb, hc=hc: e.matmul(self.ps[b][:, :], hT[:, kc, hc:hc + 128], wb[:, kc, :],
                                                                      start=(kc == 0), stop=(kc == KC - 1)),
                         reads=[("rw", sl), hk], writes=[("ps", b)])
                sg = stg[scnt % 4]
                sk = ("stg", scnt % 4)
                slot = "rst%d" % (scnt % 4)
                scnt += 1
                fn = AF.Silu if is_g else AF.Copy
                P.act(lambda e, b=b, sg=sg, fn=fn: e.activation(sg[:, :], self.ps[b][:, :], fn), reads=[("ps", b)], writes=[sk])
                dst = (self.r_g if is_g else self.r_v)[ch][:, (g % 4) * 512:(g % 4 + 1) * 512]
                P.dma("sp", dst, sg[:, :], slot, reads=[sk])
        P.barrier()
        self.aoff = 0

    def ret_zero(self):
        P = self.P
        P.dve(lambda e: e.memset(self.r_Sf[:, :, :, :], 0.0), writes=[("S", 0)])
        P.dve(lambda e: e.memset(self.r_Sb[:, :, :, :], 0.0), writes=[("S", 1)])

    def ret_sweep(self, d, chunks, store):
        P = self.P
        if not hasattr(self, "_rs_bufs") or self._rs_bufs[0] != self.aoff_epoch:
            kTc = [self.alloc([128, 8, 128], BF16) for _ in range(2)]
            vc = [self.alloc([128, 2048], BF16) for _ in range(2)]
            kd = [self.alloc([128, 8, 128], BF16) for _ in range(2)]
            sbf = [self.alloc([128, 4096], BF16) for _ in range(2)]
            self._rs_bufs = (self.aoff_epoch, kTc, vc, kd, sbf, [0])
        _, kTc_, vc_, kd_, sbf_, cnt = self._rs_bufs
        S = self.r_Sf if d == 0 else self.r_Sb
        for n in chunks:
            sl = cnt[0] % 2
            cnt[0] += 1
            kTc, vc, kd, sbf = kTc_[sl], vc_[sl], kd_[sl], sbf_[sl]
            P.dma("sp", kTc[:, :, :], self.r_kT[:, :, n * 128:(n + 1) * 128], "rk%d" % sl, writes=[("rk", sl)])
            P.dma("sp", vc[:, :], self.r_v[n], "rv%d" % sl, writes=[("rv", sl)])
            if store:
                P.act(lambda e, sbf=sbf, S=S: e.activation(sbf[:, :], S[:, :, :, :].rearrange("p a b c -> p (a b c)"), AF.Copy),
                      reads=[("S", d)], writes=[("sbf", sl)])
                P.dma("sp", self.r_S[d, n], sbf[:, :], "rss%d" % sl, reads=[("sbf", sl)])
            bt = self.nextps()
            ptb = self.ps[bt][:, :].bitcast(BF16)
            for c in range(8):
                P.pe(lambda e, ptb=ptb, c=c, kTc=kTc: e.transpose(ptb[:, c * 128:(c + 1) * 128], kTc[:, c, :], self.identb[:]),
                     reads=[("rk", sl), "identb"], writes=[("ps", bt)])
            for h in range(4):
                col = h + 4 * d
                P.act(lambda e, ptb=ptb, h=h, col=col, kd=kd: e.activation(
                    kd[:, 2 * h:2 * h + 2, :], ptb[:, h * 256:(h + 1) * 256].rearrange("p (a b) -> p a b", b=128), AF.Copy,
                    scale=self.r_wk[:, col:col + 1]), reads=[("ps", bt)], writes=[("kd", sl)])
            for h in range(4):
                col = h + 4 * d
                for dc in range(2):
                    b = self.nextps()
                    P.pe(lambda e, b=b, h=h, dc=dc, kd=kd, vc=vc: e.matmul(self.ps[b][:, :], kd[:, 2 * h + dc, :], vc[:, h * 512:(h + 1) * 512],
                                                                          start=True, stop=True), reads=[("kd", sl), ("rv", sl)], writes=[("ps", b)])
                    P.dve(lambda e, b=b, h=h, dc=dc, col=col, S=S: e.scalar_tensor_tensor(
                        S[:, h, dc, :], S[:, h, dc, :], self.r_gC[:, col:col + 1], self.ps[b][:, :], ALU.mult, ALU.add),
                        reads=[("ps", b), ("S", d)], writes=[("S", d)])

    def ret_combine(self, Lall):
        P = self.P
        stg = [self.alloc([128, 4096], F32) for _ in range(2)]
        cnt = 0
        for d in range(2):
            S = self.r_Sf if d == 0 else self.r_Sb
            co = self.r_cF if d == 0 else self.r_cB
            for h in range(4):
                col = h + 4 * d
                P.dve(lambda e, S=S, h=h, col=col: e.tensor_scalar(S[:, h, :, :], S[:, h, :, :], self.r_cc[:, col:col + 1], None, ALU.mult),
                      reads=[("S", d)], writes=[("S", d)])
            for j in range(NCORES):
                sl = cnt % 2
                cnt += 1
                P.dma("sp", stg[sl][:, :], Lall[j, d], "rl%d" % sl, writes=[("rl", sl)])
                lv = stg[sl].rearrange("p (a b c) -> p a b c", b=2, c=512)
                for h in range(4):
                    P.dve(lambda e, S=S, h=h, j=j, lv=lv, co=co: e.scalar_tensor_tensor(
                        S[:, h, :, :], lv[:, h, :, :], co[:, h, j:j + 1], S[:, h, :, :], ALU.mult, ALU.add),
                        reads=[("rl", sl), ("S", d)], writes=[("S", d)])

    def ret_out(self, with_ctx):
        P = self.P
        P.barrier()
        self.aoff = 0
        self.atop = self.r_keep_top
        wout = self.alloc([128, 16, D], BF16)
        yT = self.alloc([128, 16, 512], BF16)
        P.dma("pool", wout[:, :, :], self.W("ret_w_out")[0].rearrange("(kc p) n -> p kc n", p=128), "rwo", writes=["rwo"])
        L = []
        for sl in range(2):
            L.append(dict(q=self.alloc([128, 8, 128], BF16), k=self.alloc([128, 8, 128], BF16), v=self.alloc([128, 2048], BF16),
                          g=self.alloc([128, 2048], BF16), sf=self.alloc([128, 4, 2, 512], BF16), sb=self.alloc([128, 4, 2, 512], BF16)))
        qf = self.alloc([128, 8, 128], BF16)
        qb = self.alloc([128, 8, 128], BF16)
        PT = self.alloc([128, 4, 128], BF16)
        y = self.alloc([128, 2048], BF16)
        junk = self.alloc([128, 512], F32)
        ss = self.alloc([128, 4], F32)
        chunks = list(range(NCH)) if with_ctx else list(range(2, NCH))
        ps = self.ps
        for idx, n in enumerate(chunks):
            sl = idx % 2
            B = L[sl]
            P.dma("sp", B["q"][:, :, :], self.r_qT[:, :, n * 128:(n + 1) * 128], "oq%d" % sl, writes=[("oq", sl)])
            P.dma("sp", B["k"][:, :, :], self.r_kT[:, :, n * 128:(n + 1) * 128], "ok%d" % sl, writes=[("ok", sl)])
            P.dma("sp", B["v"][:, :], self.r_v[n], "ov%d" % sl, writes=[("ov", sl)])
            P.dma("sp", B["g"][:, :], self.r_g[n], "og%d" % sl, writes=[("og", sl)])
            P.dma("sp", B["sf"][:, :, :, :].rearrange("p a b c -> p (a b c)"), self.r_S[0, n], "osf%d" % sl, writes=[("osf", sl)])
            P.dma("sp", B["sb"][:, :, :, :].rearrange("p a b c -> p (a b c)"), self.r_S[1, n], "osb%d" % sl, writes=[("osb", sl)])
            for h in range(4):
                P.dve(lambda e, B=B, h=h: e.tensor_tensor(qf[:, 2 * h:2 * h + 2, :], B["q"][:, 2 * h:2 * h + 2, :],
                                                          self.r_Df[:, h, :].unsqueeze(1).to_broadcast([128, 2, 128]), ALU.mult),
                      reads=[("oq", sl)], writes=["qf"])
                P.dve(lambda e, B=B, h=h: e.tensor_tensor(qb[:, 2 * h:2 * h + 2, :], B["q"][:, 2 * h:2 * h + 2, :],
                                                          self.r_Db[:, h, :].unsqueeze(1).to_broadcast([128, 2, 128]), ALU.mult),
                      reads=[("oq", sl)], writes=["qb"])
            for h in range(4):
                for dc in range(2):
                    P.pe(lambda e, B=B, h=h, dc=dc: e.matmul(ps[4][:, h * 128:(h + 1) * 128], B["k"][:, 2 * h + dc, :], B["q"][:, 2 * h + dc, :],
                                                              start=(dc == 0), stop=(dc == 1)), reads=[("ok", sl), ("oq", sl)], writes=[("ps", 4)])
            P.dve(lambda e: e.tensor_tensor(PT[:, :, :], ps[4][:, :].rearrange("p (a b) -> p a b", b=128), self.r_Dc[:, :, :], ALU.mult),
                  reads=[("ps", 4)], writes=["PT"])
            P.dve(lambda e: e.memset(ss[:, :], 0.0), writes=["ss"])
            for h in range(4):
                P.pe(lambda e, B=B, h=h: e.matmul(ps[h][:, :], PT[:, h, :], B["v"][:, h * 512:(h + 1) * 512], start=True, stop=False),
                     reads=["PT", ("ov", sl)], writes=[("ps", h)])
                for dc in range(2):
                    P.pe(lambda e, B=B, h=h, dc=dc: e.matmul(ps[h][:, :], qf[:, 2 * h + dc, :], B["sf"][:, h, dc, :], start=False, stop=False),
                         reads=["qf", ("osf", sl)], writes=[("ps", h)])
                for dc in range(2):
                    P.pe(lambda e, B=B, h=h, dc=dc: e.matmul(ps[h][:, :], qb[:, 2 * h + dc, :], B["sb"][:, h, dc, :], start=False, stop=(dc == 1)),
                         reads=["qb", ("osb", sl)], writes=[("ps", h)])
                P.act(lambda e, h=h: e.activation(junk[:, :], ps[h][:, :], AF.Square, accum_out=ss[:, h:h + 1]),
                      reads=[("ps", h), "ss"], writes=["junk", "ss"])
            P.act(lambda e: e.activation(ss[:, :], ss[:, :], AF.Sqrt, bias=self.epsb[:, 0:1], scale=1.0 / 512), reads=["ss"], writes=["ss"])
            P.dve(lambda e: e.reciprocal(ss[:, :], ss[:, :]), reads=["ss"], writes=["ss"])
            for h in range(4):
                P.dve(lambda e, B=B, h=h: e.scalar_tensor_tensor(y[:, h * 512:(h + 1) * 512], ps[h][:, :], ss[:, h:h + 1],
                                                                 B["g"][:, h * 512:(h + 1) * 512], ALU.mult, ALU.mult),
                      reads=[("ps", h), "ss", ("og", sl)], writes=["y"])
            if n < 2:
                blk, slot, nblk = ("c", 0, 256, 1), n, 2
            else:
                blk, slot, nblk = ("x", ((n - 2) // 4) * 512, 512, 0), (n - 2) % 4, 4
            ptb = ps[5][:, :].bitcast(BF16)
            for half in range(2):
                for c in range(8):
                    fc = half * 8 + c
                    P.pe(lambda e, c=c, fc=fc: e.transpose(ptb[:, c * 128:(c + 1) * 128], y[:, fc * 128:(fc + 1) * 128], self.identb[:]),
                         reads=["y", "identb"], writes=[("ps", 5)])
                P.act(lambda e, half=half, slot=slot: e.activation(yT[:, half * 8:half * 8 + 8, slot * 128:(slot + 1) * 128],
                                                                    ptb[:, 0:1024].rearrange("p (a b) -> p a b", b=128), AF.Copy),
                      reads=[("ps", 5)], writes=["yT"])
            if slot == nblk - 1:
                s, t0, nn, st = blk
                xs = self.src(s)
                for dc in range(KC):
                    bo = 6 + dc % 2
                    for fc in range(16):
                        P.pe(lambda e, bo=bo, fc=fc, dc=dc, nn=nn: e.matmul(ps[bo][:, 0:nn], wout[:, fc, dc * 128:(dc + 1) * 128], yT[:, fc, 0:nn],
                                                                            start=(fc == 0), stop=(fc == 15)), reads=["rwo", "yT"], writes=[("ps", bo)])
                    m = 2 * 8 + dc
                    P.dve(lambda e, bo=bo, nn=nn, xs=xs, dc=dc, t0=t0, m=m, st=st: e.scalar_tensor_tensor(
                        xs[:, dc, t0:t0 + nn], ps[bo][:, 0:nn], self.mods[:, m, st:st + 1], xs[:, dc, t0:t0 + nn],
                        ALU.mult, ALU.add), reads=[("ps", bo), ("mods", m)], writes=[(s, "T")])
        self.areset()

    def linear_fm(self, W, kchunks, groups, blocks, rhs_fn, epi, wslots, wtag):
        P = self.P
        for gi, grp in enumerate(groups):
            sl = gi % len(wslots)
            wb = wslots[sl]
            c = 0
            for (c0, nc_) in grp:
                P.dma("pool", wb[:, 0:kchunks, c:c + nc_],
                      W[:, c0:c0 + nc_].rearrange("(kc p) n -> p kc n", p=128),
                      "%s%d" % (wtag, sl), writes=[(wtag, sl)])
                c += nc_
            nch = c // 128
            for blk in blocks:
                for ci in range(nch):
                    b = self.nextps()
                    n = blk[2]
                    for kc in range(kchunks):
                        rhs, rk = rhs_fn(blk, kc)
                        P.pe(lambda e, b=b, n=n, wb=wb, kc=kc, ci=ci, rhs=rhs: e.matmul(
                            self.ps[b][:, 0:n], wb[:, kc, ci * 128:(ci + 1) * 128], rhs,
                            start=(kc == 0), stop=(kc == kchunks - 1)),
                            reads=[(wtag, sl)] + rk, writes=[("ps", b)])
                    epi(gi, ci, blk, b)

    def adaln(self, i):
        P = self.P
        self.aoff = 0
        wsl = [self.alloc([128, KC, 512], BF16) for _ in range(3)]
        o, n = PV_LAY["cond"]
        condv = self.pv[:, o:o + n].rearrange("p (a b) -> p a b", b=2)
        P.act(lambda e: e.activation(self.scond[:], condv, AF.Silu), reads=["pv"], writes=["scond"])
        ob, _ = PV_LAY[f"ada_b{i}"]
        groups = [[(g * 512, 512)] for g in range(12)]

        def rhs_fn(blk, kc):
            return self.scond[:, kc, :], ["scond"]

        def epi(gi, ci, blk, b):
            m = gi * 4 + ci
            P.dve(lambda e, b=b, m=m: e.tensor_scalar(self.mods[:, m, :], self.ps[b][:, 0:2],
                                                       self.pv[:, ob + m:ob + m + 1], None, ALU.add),
                  reads=[("ps", b), "pv"], writes=[("mods", m)])

        self.linear_fm(self.W(f"ada_w{i}"), KC, groups, [("cond", 0, 2, 0)], rhs_fn, epi, wsl, "wada")
        for which, (m, gname) in enumerate([(1, f"nmix{i}"), (4, f"nffn{i}")]):
            og, _ = PV_LAY[gname]
            for kc in range(KC):
                P.dve(lambda e, which=which, m=m, kc=kc, og=og: e.tensor_scalar(
                    self.modA[:, which, kc, :], self.mods[:, m * 8 + kc, :], 1.0, self.pv[:, og + kc:og + kc + 1],
                    ALU.add, ALU.mult), reads=[("mods", m * 8 + kc), "pv"], writes=[("modA", which, kc)])
        self.areset()

    def modulate(self, which, blocks, hT_of, sq, tmp):
        P = self.P
        mshift = 0 if which == 0 else 3
        for blk in blocks:
            s, t0, n, st = blk
            xs = self.src(s)
            xv = xs[:, :, t0:t0 + n]
            xkey = (s, "T")
            P.act(lambda e, xv=xv, n=n: e.activation(sq[:, :, 0:n], xv, AF.Square), reads=[xkey], writes=["sq"])
            b = self.nextps()
            for kc in range(KC):
                P.pe(lambda e, b=b, kc=kc, n=n: e.matmul(self.ps[b][:, 0:n], self.onesb[:], sq[:, kc, 0:n],
                                                          start=(kc == 0), stop=(kc == KC - 1)),
                     reads=["sq", "onesb"], writes=[("ps", b)])
            rstd = tmp[:, 0, 0:n]
            P.act(lambda e, b=b, n=n, rstd=rstd: e.activation(rstd, self.ps[b][:, 0:n], AF.Sqrt, bias=self.epsb[:, 0:1], scale=1.0 / D),
                  reads=[("ps", b), "epsb"], writes=["rstd"])
            P.dve(lambda e, rstd=rstd: e.reciprocal(rstd, rstd), reads=["rstd"], writes=["rstd"])
            dst, dkey = hT_of(blk)
            for kc in range(KC):
                t2 = tmp[:, 1 + kc % 2, 0:n]
                t2k = ("t2", kc % 2)
                P.dve(lambda e, kc=kc, xs=xs, t0=t0, n=n, t2=t2, rstd=rstd, st=st: e.scalar_tensor_tensor(
                    t2, xs[:, kc, t0:t0 + n], self.modA[:, which, kc, st:st + 1], rstd, ALU.mult, ALU.mult),
                    reads=[xkey, "rstd", ("modA", which, kc)], writes=[t2k])
                m = mshift * 8 + kc
                P.act(lambda e, kc=kc, n=n, t2=t2, dst=dst, m=m, st=st: e.activation(
                    dst[:, kc, 0:n], t2, AF.Identity, bias=self.mods[:, m, st:st + 1], scale=1.0),
                    reads=[t2k, ("mods", m)], writes=[dkey])

    def ffn(self, i, blocks):
        P = self.P
        self.aoff = 0
        ntok = sum(b[2] for b in blocks)
        hT = self.alloc([128, KC, ntok], BF16)
        sq = self.alloc([128, KC, 512], BF16)
        tmp = self.alloc([128, 3, 512], F32)
        offs = {}
        o = 0
        for blk in blocks:
            offs[blk] = o
            o += blk[2]

        def hT_of(blk):
            return hT[:, :, offs[blk]:offs[blk] + blk[2]], ("hT", blk[0], blk[1])

        self.modulate(1, blocks, hT_of, sq, tmp)
        wgu = [self.alloc([128, KC, 1024], BF16) for _ in range(2)]
        wdn = [self.alloc([128, 4, 1024], BF16) for _ in range(2)]
        actT = [self.alloc([128, 4, 512], BF16) for _ in range(2)]
        sg = [self.alloc([128, 512], F32) for _ in range(2)]
        Wgu = self.W(f"ffn_w_gu{i}")
        Wdn = self.W(f"ffn_w_down{i}")
        ngrp = (FHC + 3) // 4
        cnt = 0
        for g in range(ngrp):
            j0 = g * 4
            nj = min(4, FHC - j0)
            sl = g % 2
            P.dma("pool", wgu[sl][:, :, 0:nj * 128], Wgu[:, j0 * 128:(j0 + nj) * 128].rearrange("(kc p) n -> p kc n", p=128),
                  "wgu%d" % sl, writes=[("wgu", sl)])
            P.dma("pool", wgu[sl][:, :, 512:512 + nj * 128],
                  Wgu[:, FH + j0 * 128:FH + (j0 + nj) * 128].rearrange("(kc p) n -> p kc n", p=128),
                  "wgu%d" % sl, writes=[("wgu", sl)])
            P.dma("pool", wdn[sl][:, 0:nj, :], Wdn[j0 * 128:(j0 + nj) * 128, :].rearrange("(kc p) n -> p kc n", p=128),
                  "wdn%d" % sl, writes=[("wdn", sl)])
            for blk in blocks:
                s, t0, n, st = blk
                asl = cnt % 2
                cnt += 1
                hv = hT[:, :, offs[blk]:offs[blk] + n]
                hk = ("hT", blk[0], blk[1])
                for j in range(nj):
                    bg = self.nextps()
                    bu = self.nextps()
                    for kc in range(KC):
                        P.pe(lambda e, bg=bg, n=n, sl=sl, kc=kc, j=j, hv=hv: e.matmul(
                            self.ps[bg][:, 0:n], wgu[sl][:, kc, j * 128:(j + 1) * 128], hv[:, kc, :],
                            start=(kc == 0), stop=(kc == KC - 1)), reads=[("wgu", sl), hk], writes=[("ps", bg)])
                    for kc in range(KC):
                        P.pe(lambda e, bu=bu, n=n, sl=sl, kc=kc, j=j, hv=hv: e.matmul(
                            self.ps[bu][:, 0:n], wgu[sl][:, kc, 512 + j * 128:512 + (j + 1) * 128], hv[:, kc, :],
                            start=(kc == 0), stop=(kc == KC - 1)), reads=[("wgu", sl), hk], writes=[("ps", bu)])
                    gs = sg[j % 2]
                    P.act(lambda e, bg=bg, n=n, gs=gs: e.activation(gs[:, 0:n], self.ps[bg][:, 0:n], AF.Silu),
                          reads=[("ps", bg)], writes=[("sg", j % 2)])
                    P.dve(lambda e, bu=bu, n=n, gs=gs, asl=asl, j=j: e.tensor_tensor(
                        actT[asl][:, j, 0:n], gs[:, 0:n], self.ps[bu][:, 0:n], ALU.mult),
                        reads=[("sg", j % 2), ("ps", bu)], writes=[("actT", asl)])
                xs = self.src(s)
                for dc in range(KC):
                    bo = self.nextps()
                    for j in range(nj):
                        P.pe(lambda e, bo=bo, n=n, sl=sl, j=j, dc=dc, asl=asl: e.matmul(
                            self.ps[bo][:, 0:n], wdn[sl][:, j, dc * 128:(dc + 1) * 128], actT[asl][:, j, 0:n],
                            start=(j == 0), stop=(j == nj - 1)), reads=[("wdn", sl), ("actT", asl)], writes=[("ps", bo)])
                    m = 5 * 8 + dc
                    P.dve(lambda e, bo=bo, n=n, xs=xs, dc=dc, t0=t0, m=m, st=st: e.scalar_tensor_tensor(
                        xs[:, dc, t0:t0 + n], self.ps[bo][:, 0:n], self.mods[:, m, st:st + 1], xs[:, dc, t0:t0 + n],
                        ALU.mult, ALU.add), reads=[("ps", bo), ("mods", m)], writes=[(s, "T")])
        self.areset()

    def build(self):
        want_ctx = True
        kinds = [st[0] for st in self.steps]
        if any(st[0] == "att_kv" and st[2] for st in self.steps):
            self.att_bnd = self.nc.dram_tensor("att_bnd", [128, 1032], BF16, kind="ExternalOutput").ap()
        self.extra_outs = []
        if "ret_a" in kinds or "ret_b" in kinds:
            self.rt_in = self._di("rt", [128, RT_N])
        if "ret_a" in kinds:
            self.L_out = self.nc.dram_tensor("L_out", [2, 128, 4096], F32, kind="ExternalOutput").ap()
        if "ret_b" in kinds:
            self.Lall = self._di("Lall", [NCORES, 2, 128, 4096])
        if "att_halo" in kinds:
            self.att_halo = self.nc.dram_tensor("att_halo", [128, 1032], BF16, kind="ExternalInput").ap()
        self.prologue()
        for st in self.steps:
            kind = st[0]
            if kind == "ada":
                self.adaln(st[1])
            elif kind == "halo":
                self.load_halo(self.halo_in)
            elif kind == "conv":
                self.conv(st[1], st[2], st[3])
            elif kind == "att_kv":
                self.att_kv(st[1], self.att_bnd if st[2] else None)
            elif kind == "att_halo":
                self.att_load_halo(self.att_halo)
            elif kind == "att_q":
                self.att_q(st[1], st[2])
            elif kind == "ret_a":
                self.ret_tables()
                self.ret_proj(True)
                self.ret_zero()
                lat = list(range(2, NCH))
                self.ret_sweep(0, lat, False)
                self.ret_sweep(1, lat[::-1], False)
                self.P.dma("sp", self.L_out[0], self.r_Sf[:, :, :, :].rearrange("p a b c -> p (a b c)"), "lout", reads=[("S", 0)])
                self.P.dma("sp", self.L_out[1], self.r_Sb[:, :, :, :].rearrange("p a b c -> p (a b c)"), "lout", reads=[("S", 1)])
                self.extra_outs.append("lout")
                self.areset()
            elif kind == "ret_b":
                self.ret_tables()
                self.ret_proj(False)
                self.ret_zero()
                self.ret_sweep(0, [0, 1], True)
                self.ret_sweep(1, [1, 0], True)
                self.ret_combine(self.Lall)
                lat = list(range(2, NCH))
                self.ret_sweep(0, lat, True)
                self.ret_sweep(1, lat[::-1], True)
                self.ret_out(True)
            elif kind == "ffn":
                self.ffn(st[1], self.blocks if st[2] else self.blocks[:4])
        outs = self.epilogue(want_ctx)
        if any(st[0] == "att_kv" and st[2] for st in self.steps):
            outs.append("bnd")
        self.P.emit(final_waits=outs + self.extra_outs)
        return self.nc


def make_cst():
    cst = np.zeros((128, 3, 128), np.float32)
    p = np.arange(128)
    cst[:, 0, :] = (p[:, None] // 64 == p[None, :] // 64)
    sw16 = (p // 32) * 32 + (p % 32 + 16) % 32
    cst[p, 1, sw16] = 1.0
    sw64 = (p + 64) % 128
    cst[p, 2, sw64] = 1.0
    return cst.reshape(128, 384)


def pack_pc(c):
    pc = np.zeros((128, PC_N), np.float32)
    pc[:, 0:16] = 0.0 if c == 0 else 1.0
    pc[:, 16:32] = 0.0 if c == NCORES - 1 else 1.0
    j = np.arange(128)[:, None]
    a = np.arange(128)[None, :]
    tril = (j >= a).astype(np.float32)
    trir = (j <= a).astype(np.float32)
    pc[:, PC_TRIL:PC_TRIL + 128] = tril
    pc[:, PC_TRIR:PC_TRIR + 128] = trir
    pc[:, PC_ML0:PC_ML0 + 128] = tril * (0.0 if c == 0 else 1.0)
    pc[:, PC_MR15:PC_MR15 + 128] = trir * (0.0 if c == NCORES - 1 else 1.0)
    rows = (c * (T // 64) + np.arange(T // 64)).astype(np.float64)
    cols = np.arange(64).astype(np.float64)
    p = np.arange(128)
    q = p % 64
    inv16 = 10000.0 ** (-(q % 16) / 16.0)
    sgn = np.where((q % 32) < 16, -1.0, 1.0)
    isrow = q < 32
    o = PC_AROPE
    ang_r = rows[None, :] * inv16[:, None]
    ang_c = cols[None, :] * inv16[:, None]
    pc[:, o:o + 32] = np.where(isrow[:, None], np.cos(ang_r), 0.0)
    pc[:, o + 32:o + 64] = np.where(isrow[:, None], sgn[:, None] * np.sin(ang_r), 0.0)
    pc[:, o + 64:o + 128] = np.where(~isrow[:, None], np.cos(ang_c), 0.0)
    pc[:, o + 128:o + 192] = np.where(~isrow[:, None], sgn[:, None] * np.sin(ang_c), 0.0)
    inv64 = 10000.0 ** (-(p % 64) / 64.0)
    sg = np.where(p < 64, -1.0, 1.0)
    o = PC_RROPE
    ang_r = rows[None, :] * inv64[:, None]
    ang_c = cols[None, :] * inv64[:, None]
    pc[:, o:o + 32] = np.cos(ang_r)
    pc[:, o + 32:o + 64] = sg[:, None] * np.sin(ang_r)
    pc[:, o + 64:o + 128] = np.cos(ang_c)
    pc[:, o + 128:o + 192] = sg[:, None] * np.sin(ang_c)
    return pc


def pack_rt(c):
    rt = np.zeros((128, RT_N), np.float32)
    m = np.arange(128)[:, None].astype(np.float32)
    cc = np.arange(128)[None, :].astype(np.float32)
    rt[:, RT_RELP:RT_RELP + 128] = np.maximum(cc - m, 0)
    rt[:, RT_RELN:RT_RELN + 128] = np.maximum(m - cc, 0)
    rt[:, RT_MGE:RT_MGE + 128] = (cc >= m)
    rt[:, RT_MLE:RT_MLE + 128] = (m >= cc)
    rt[:, RT_C1:RT_C1 + 128] = cc + 1
    rt[:, RT_128C:RT_128C + 128] = 128 - cc
    rt[:, RT_COLA] = 127 - np.arange(128)
    rt[:, RT_COLB] = np.arange(128)
    for j in range(NCORES):
        if j < c:
            rt[:, RT_DISTF + j] = T * (c - 1 - j)
            rt[:, RT_MASKF + j] = 1.0
        if j > c:
            rt[:, RT_DISTB + j] = T * (j - c - 1)
            rt[:, RT_MASKB + j] = 1.0
    rt[:, RT_CTXF] = T * c
    rt[:, RT_CTXB] = T * (NCORES - 1 - c)
    return rt


def make_in_maps(inp, builder, xs=None, ctx=None, halos=None):
    pv = pack_pv(inp)
    ident = np.eye(128, dtype=np.float32)
    if xs is None:
        x = np.asarray(inp["x"], np.float32)[0]
        xs = [np.ascontiguousarray(x[c * T:(c + 1) * T]) for c in range(NCORES)]
    if ctx is None:
        ctx = np.ascontiguousarray(np.asarray(inp["ctx"], np.float32)[0])
    if halos is None:
        halos = []
        z16 = np.zeros((16, D), np.float32)
        for c in range(NCORES):
            l = xs[c - 1][-16:] if c > 0 else z16
            r = xs[c + 1][:16] if c < NCORES - 1 else z16
            halos.append(np.ascontiguousarray(np.concatenate([l, r], 0)))
    maps = []
    for c in range(NCORES):
        m = {"x_in": xs[c], "ctx_in": ctx, "pv": pv, "pc": pack_pc(c), "ident": ident, "halo_in": halos[c], "cst": make_cst()}
        if hasattr(builder, "rt_in"):
            m["rt"] = pack_rt(c)
        for name in builder.wts:
            if name[:-1] in ("ada_w", "ffn_w_gu", "ffn_w_down"):
                m[name] = np.ascontiguousarray(np.asarray(inp[name[:-1]], np.float32)[int(name[-1])])
            else:
                m[name] = np.asarray(inp[name], np.float32)
        maps.append(m)
    return maps


def _route_att(bnd):
    halos = []
    for c in range(NCORES):
        h = np.zeros_like(bnd[0])
        if c > 0:
            p = bnd[c - 1]
            for ch in range(2):
                h[:, ch * 256:ch * 256 + 128] = p[:, ch * 256 + 128:ch * 256 + 256]
            h[:, 512:772] = p[:, 772:1032]
        if c < NCORES - 1:
            p = bnd[c + 1]
            for ch in range(2):
                h[:, ch * 256 + 128:ch * 256 + 256] = p[:, ch * 256:ch * 256 + 128]
            h[:, 772:1032] = p[:, 512:772]
        halos.append(h)
    return halos


def _launch(steps, inp, xs=None, ctx=None, extra=None):
    B = Builder(steps)
    nc = B.build()
    maps = make_in_maps(inp, B, xs=xs, ctx=ctx)
    if extra:
        for c in range(NCORES):
            maps[c].update(extra(c))
    res = run_bass_kernel_spmd(nc, maps, core_ids=list(range(NCORES)))
    return res.results


def kernel_unfused(**inputs):
    inp = {k: np.asarray(v) for k, v in inputs.items()}
    r = _launch([("ada", 0), ("halo",), ("conv", 0, 0, True), ("ffn", 0, True), ("ada", 1), ("ret_a",)], inp)
    xs = [np.asarray(q["x_out"]) for q in r]
    ctx = np.asarray(r[0]["ctx_out"])
    Lall = np.stack([np.asarray(q["L_out"]) for q in r], 0)
    r = _launch([("ada", 1), ("ret_b",), ("ffn", 1, True), ("ada", 2), ("att_kv", 2, True)], inp, xs=xs, ctx=ctx,
                extra=lambda c: {"Lall": Lall})
    xs = [np.asarray(q["x_out"]) for q in r]
    ctx = np.asarray(r[0]["ctx_out"])
    halos = _route_att([np.asarray(q["att_bnd"]) for q in r])
    r = _launch([("ada", 2), ("att_kv", 2, False), ("att_halo",), ("att_q", 2, True), ("ffn", 2, True)], inp, xs=xs, ctx=ctx,
                extra=lambda c: {"att_halo": halos[c]})
    xs = [np.asarray(q["x_out"]) for q in r]
    ctx = np.asarray(r[0]["ctx_out"])
    r = _launch([("ada", 3), ("halo",), ("conv", 3, 1, False), ("ffn", 3, False)], inp, xs=xs, ctx=ctx)
    out = np.concatenate([np.asarray(q["x_out"]) for q in r], 0)[None]
    return out.astype(np.float32)


def kernel(**inputs):
    return kernel_unfused(**inputs)
```
